# Optimizing a Trainium2 kernel written in Bass

```python
import jax, jax.numpy as jnp
from jax import lax
import numpy as np

D_MODEL = 1024
BATCH = 4
SEQ = 8192
DEPTH = 2
DEC_BATCH = 8
DEC_SEQ = 16
PAST_LEN = 2048

CHUNK = 64
N_A_LAYERS = max(1, DEPTH // 2)
N_B_LAYERS = DEPTH - N_A_LAYERS
HEAD_DIM = 64
FOX_HEADS = D_MODEL // HEAD_DIM
SWA_Q_HEADS = D_MODEL // HEAD_DIM
SWA_KV_HEADS = SWA_Q_HEADS // 4
SWA_GROUP = SWA_Q_HEADS // SWA_KV_HEADS
WINDOW = 128
WIN_CHUNKS = WINDOW // CHUNK
Q_BLOCK = 128
D_FF = -(-(8 * D_MODEL) // (3 * 256)) * 256
PLE_DIM = 256
EPS = 1e-6
SCALE = HEAD_DIM ** -0.5

kernel_name = 'yoco_fox_swa_sink_streaming_step'


def _rms(x, g):
    x32 = x.astype(jnp.float32)
    y = x32 * lax.rsqrt(jnp.mean(x32 * x32, axis=-1, keepdims=True) + EPS)
    return (y * g.astype(jnp.float32)).astype(x.dtype)


def _swiglu(a, w_gate, w_up, w_down):
    return (jax.nn.silu(a @ w_gate) * (a @ w_up)) @ w_down


def _alibi_slopes():
    n = SWA_Q_HEADS
    s = jnp.exp2(-8.0 * jnp.arange(1, n + 1, dtype=jnp.float32) / n)
    return s.reshape(SWA_KV_HEADS, SWA_GROUP)


def _band_mask_dist(qpos, kpos):
    dc = qpos // CHUNK - kpos // CHUNK
    mask = (dc >= 0) & (dc <= WIN_CHUNKS) & (kpos >= 0)
    dist = jnp.abs(qpos - kpos).astype(jnp.float32)
    return mask, dist


def _sink_softmax(s, sink):
    m = jnp.maximum(jnp.max(s, axis=-1, keepdims=True), sink)
    e = jnp.exp(s - m)
    return e / (jnp.sum(e, axis=-1, keepdims=True) + jnp.exp(sink - m))


def _fox_prompt(q, k, v, logf):
    b, s_len, h, hd = q.shape
    nb = s_len // Q_BLOCK
    F = jnp.cumsum(logf, axis=1)
    Fk = jnp.swapaxes(F, 1, 2)[:, :, None, :]
    qb = jnp.moveaxis(q.reshape(b, nb, Q_BLOCK, h, hd), 1, 0)
    Fb = jnp.moveaxis(F.reshape(b, nb, Q_BLOCK, h), 1, 0)
    kpos = jnp.arange(s_len)

    def block(args):
        qi, Fi, bi = args
        sc = jnp.einsum('bqhd,bkhd->bhqk', qi, k).astype(jnp.float32) * SCALE
        sc = sc + jnp.swapaxes(Fi, 1, 2)[..., None] - Fk
        qpos = bi * Q_BLOCK + jnp.arange(Q_BLOCK)
        sc = jnp.where(kpos[None, :] <= qpos[:, None], sc, -jnp.inf)
        w = jax.nn.softmax(sc, axis=-1)
        return jnp.einsum('bhqk,bkhd->bqhd', w.astype(v.dtype), v)

    out = lax.map(block, (qb, Fb, jnp.arange(nb)))
    return jnp.moveaxis(out, 0, 1).reshape(b, s_len, h, hd)


def _fox_sample(q, k, v, logf, ck, cv, clogf):
    t = q.shape[1]
    p_len = ck.shape[1]
    k_all = jnp.concatenate([ck, k.astype(ck.dtype)], axis=1)
    v_all = jnp.concatenate([cv, v.astype(cv.dtype)], axis=1)
    F = jnp.cumsum(jnp.concatenate([clogf.astype(jnp.float32), logf], axis=1), axis=1)
    Fq = F[:, p_len:]
    sc = jnp.einsum('bthd,bshd->bhts', q, k_all).astype(jnp.float32) * SCALE
    sc = sc + jnp.swapaxes(Fq, 1, 2)[..., None] - jnp.swapaxes(F, 1, 2)[:, :, None, :]
    qpos = p_len + jnp.arange(t)
    kpos = jnp.arange(p_len + t)
    sc = jnp.where(kpos[None, :] <= qpos[:, None], sc, -jnp.inf)
    w = jax.nn.softmax(sc, axis=-1)
    return jnp.einsum('bhts,bshd->bthd', w.astype(v_all.dtype), v_all)


def _swa_prompt(q, k, v, sink, slopes):
    b, s_len, _, hd = q.shape
    nb = s_len // WINDOW
    pad = ((0, 0), (WINDOW, 0), (0, 0), (0, 0))
    kp = jnp.pad(k, pad).reshape(b, nb + 1, WINDOW, SWA_KV_HEADS, hd)
    vp = jnp.pad(v, pad).reshape(b, nb + 1, WINDOW, SWA_KV_HEADS, hd)
    kb = jnp.concatenate([kp[:, :-1], kp[:, 1:]], axis=2)
    vb = jnp.concatenate([vp[:, :-1], vp[:, 1:]], axis=2)
    qg = q.reshape(b, nb, WINDOW, SWA_KV_HEADS, SWA_GROUP, hd)
    blk = jnp.arange(nb)[:, None] * WINDOW
    qpos = blk + jnp.arange(WINDOW)[None, :]
    kpos = blk - WINDOW + jnp.arange(2 * WINDOW)[None, :]
    mask, dist = _band_mask_dist(qpos[:, :, None], kpos[:, None, :])
    sc = jnp.einsum('bnqkgd,bnskd->bnkgqs', qg, kb).astype(jnp.float32) * SCALE
    sc = sc - slopes[None, None, :, :, None, None] * dist[None, :, None, None]
    sc = jnp.where(mask[None, :, None, None], sc, -jnp.inf)
    sk = sink.astype(jnp.float32).reshape(SWA_KV_HEADS, SWA_GROUP)[None, None, :, :, None, None]
    w = _sink_softmax(sc, sk)
    o = jnp.einsum('bnkgqs,bnskd->bnqkgd', w.astype(vb.dtype), vb)
    return o.reshape(b, s_len, SWA_Q_HEADS, hd)


def _swa_sample(q, k_all, v_all, sink, slopes, past_len):
    b, t, _, hd = q.shape
    l = k_all.shape[1]
    qpos = past_len + jnp.arange(t)
    kpos = past_len + t - l + jnp.arange(l)
    mask, dist = _band_mask_dist(qpos[:, None], kpos[None, :])
    qg = q.reshape(b, t, SWA_KV_HEADS, SWA_GROUP, hd)
    sc = jnp.einsum('btkgd,bskd->bkgts', qg, k_all).astype(jnp.float32) * SCALE
    sc = sc - slopes[None, :, :, None, None] * dist
    sc = jnp.where(mask, sc, -jnp.inf)
    sk = sink.astype(jnp.float32).reshape(SWA_KV_HEADS, SWA_GROUP)[None, :, :, None, None]
    w = _sink_softmax(sc, sk)
    o = jnp.einsum('bkgts,bskd->btkgd', w.astype(v_all.dtype), v_all)
    return o.reshape(b, t, SWA_Q_HEADS, hd)


def _trunk(x, p, fox_attend, swa_attend, norm_mix_pre, norm_mix_post, norm_ffn_pre,
           norm_ffn_post, fox_w_in, fox_b_f, fox_w_out, swa_w_q, swa_sinks, swa_w_out,
           kv_norm, swa_w_kv, ffn_w_gate, ffn_w_up, ffn_w_down, ple_norm, ple_w_gate,
           ple_b_gate, ple_w_proj):
    b, t, _ = x.shape
    n_qkv = 3 * FOX_HEADS * HEAD_DIM
    h = x
    fox_states = []
    kv_k = None
    kv_v = None
    for i in range(DEPTH):
        a = _rms(h, norm_mix_pre[i])
        if i < N_A_LAYERS:
            proj = a @ fox_w_in[i]
            qkv = proj[..., :n_qkv].reshape(b, t, 3, FOX_HEADS, HEAD_DIM)
            q, k, v = qkv[:, :, 0], qkv[:, :, 1], qkv[:, :, 2]
            logf = jax.nn.log_sigmoid(proj[..., n_qkv:].astype(jnp.float32)
                                      + fox_b_f[i].astype(jnp.float32))
            o = fox_attend(i, q, k, v, logf)
            fox_states.append((k, v, logf))
            mix = o.reshape(b, t, FOX_HEADS * HEAD_DIM) @ fox_w_out[i]
        else:
            j = i - N_A_LAYERS
            q = (a @ swa_w_q[j]).reshape(b, t, SWA_Q_HEADS, HEAD_DIM)
            o = swa_attend(q, kv_k, kv_v, swa_sinks[j])
            mix = o.reshape(b, t, SWA_Q_HEADS * HEAD_DIM) @ swa_w_out[j]
        h = h + _rms(mix, norm_mix_post[i])
        f = _swiglu(_rms(h, norm_ffn_pre[i]), ffn_w_gate[i], ffn_w_up[i], ffn_w_down[i])
        h = h + _rms(f, norm_ffn_post[i])
        gate = jax.nn.sigmoid(_rms(h, ple_norm[i]) @ ple_w_gate[i] + ple_b_gate[i])
        h = h + gate * (p[i] @ ple_w_proj[i])
        if i == N_A_LAYERS - 1:
            kv = (_rms(h, kv_norm) @ swa_w_kv).reshape(b, t, 2, SWA_KV_HEADS, HEAD_DIM)
            kv_k, kv_v = kv[:, :, 0], kv[:, :, 1]
    return h, fox_states, kv_k, kv_v


def setup_inputs(seed: int = 0) -> dict:
    key = jax.random.key(seed)
    it = iter(jax.random.split(key, 40))
    f32 = jnp.float32
    D = D_MODEL

    def nrm(shape, scale=1.0):
        return jax.random.normal(next(it), shape, f32) * scale

    def gain(shape):
        return 1.0 + 0.02 * nrm(shape)

    swa_buf = min(WINDOW, PAST_LEN)
    return {
        'x_prompt': nrm((BATCH, SEQ, D)),
        'x_sample': nrm((DEC_BATCH, DEC_SEQ, D)),
        'cache_fox_k': nrm((N_A_LAYERS, DEC_BATCH, PAST_LEN, FOX_HEADS, HEAD_DIM)),
        'cache_fox_v': nrm((N_A_LAYERS, DEC_BATCH, PAST_LEN, FOX_HEADS, HEAD_DIM)),
        'cache_fox_logf': jax.nn.log_sigmoid(4.0 + nrm((N_A_LAYERS, DEC_BATCH, PAST_LEN, FOX_HEADS))),
        'cache_swa_k': nrm((DEC_BATCH, swa_buf, SWA_KV_HEADS, HEAD_DIM)),
        'cache_swa_v': nrm((DEC_BATCH, swa_buf, SWA_KV_HEADS, HEAD_DIM)),
        'p_prompt': nrm((DEPTH, BATCH, SEQ, PLE_DIM)),
        'p_sample': nrm((DEPTH, DEC_BATCH, DEC_SEQ, PLE_DIM)),
        'norm_mix_pre': gain((DEPTH, D)),
        'norm_mix_post': gain((DEPTH, D)),
        'norm_ffn_pre': gain((DEPTH, D)),
        'norm_ffn_post': gain((DEPTH, D)),
        'fox_w_in': nrm((N_A_LAYERS, D, 3 * FOX_HEADS * HEAD_DIM + FOX_HEADS), D ** -0.5),
        'fox_b_f': jnp.linspace(1.0, 6.0, FOX_HEADS, dtype=f32)[None, :] + 0.01 * nrm((N_A_LAYERS, FOX_HEADS)),
        'fox_w_out': nrm((N_A_LAYERS, FOX_HEADS * HEAD_DIM, D), (FOX_HEADS * HEAD_DIM) ** -0.5),
        'swa_w_q': nrm((N_B_LAYERS, D, SWA_Q_HEADS * HEAD_DIM), D ** -0.5),
        'swa_sinks': nrm((N_B_LAYERS, SWA_Q_HEADS), 0.5),
        'swa_w_out': nrm((N_B_LAYERS, SWA_Q_HEADS * HEAD_DIM, D), (SWA_Q_HEADS * HEAD_DIM) ** -0.5),
        'kv_norm': gain((D,)),
        'swa_w_kv': nrm((D, 2 * SWA_KV_HEADS * HEAD_DIM), D ** -0.5),
        'ffn_w_gate': nrm((DEPTH, D, D_FF), D ** -0.5),
        'ffn_w_up': nrm((DEPTH, D, D_FF), D ** -0.5),
        'ffn_w_down': nrm((DEPTH, D_FF, D), D_FF ** -0.5),
        'ple_norm': gain((DEPTH, D)),
        'ple_w_gate': nrm((DEPTH, D, D), D ** -0.5),
        'ple_b_gate': nrm((DEPTH, D), 0.02),
        'ple_w_proj': nrm((DEPTH, PLE_DIM, D), PLE_DIM ** -0.5),
    }


def reference(x_prompt, x_sample, cache_fox_k, cache_fox_v, cache_fox_logf, cache_swa_k,
              cache_swa_v, p_prompt, p_sample, norm_mix_pre, norm_mix_post, norm_ffn_pre,
              norm_ffn_post, fox_w_in, fox_b_f, fox_w_out, swa_w_q, swa_sinks, swa_w_out,
              kv_norm, swa_w_kv, ffn_w_gate, ffn_w_up, ffn_w_down, ple_norm, ple_w_gate,
              ple_b_gate, ple_w_proj):
    weights = (norm_mix_pre, norm_mix_post, norm_ffn_pre, norm_ffn_post, fox_w_in, fox_b_f,
               fox_w_out, swa_w_q, swa_sinks, swa_w_out, kv_norm, swa_w_kv, ffn_w_gate,
               ffn_w_up, ffn_w_down, ple_norm, ple_w_gate, ple_b_gate, ple_w_proj)
    slopes = _alibi_slopes()
    past_len = cache_fox_k.shape[2]

    def fox_p(i, q, k, v, logf):
        return _fox_prompt(q, k, v, logf)

    def swa_p(q, k, v, sink):
        return _swa_prompt(q, k, v, sink, slopes)

    y_prompt, fox_p_states, kv_k_p, kv_v_p = _trunk(x_prompt, p_prompt, fox_p, swa_p, *weights)

    def fox_s(i, q, k, v, logf):
        return _fox_sample(q, k, v, logf, cache_fox_k[i], cache_fox_v[i], cache_fox_logf[i])

    def swa_s(q, k, v, sink):
        k_all = jnp.concatenate([cache_swa_k, k.astype(cache_swa_k.dtype)], axis=1)
        v_all = jnp.concatenate([cache_swa_v, v.astype(cache_swa_v.dtype)], axis=1)
        return _swa_sample(q, k_all, v_all, sink, slopes, past_len)

    y_sample, fox_s_states, kv_k_s, kv_v_s = _trunk(x_sample, p_sample, fox_s, swa_s, *weights)

    fox_k_prompt = jnp.stack([st[0] for st in fox_p_states])
    fox_v_prompt = jnp.stack([st[1] for st in fox_p_states])
    fox_logf_prompt = jnp.stack([st[2] for st in fox_p_states])
    fox_k_sample = jnp.stack([st[0] for st in fox_s_states])
    fox_v_sample = jnp.stack([st[1] for st in fox_s_states])
    fox_logf_sample = jnp.stack([st[2] for st in fox_s_states])
    buf_p = min(WINDOW, x_prompt.shape[1])
    swa_k_prompt = kv_k_p[:, -buf_p:]
    swa_v_prompt = kv_v_p[:, -buf_p:]
    buf_s = cache_swa_k.shape[1]
    swa_k_sample = jnp.concatenate([cache_swa_k, kv_k_s.astype(cache_swa_k.dtype)], axis=1)[:, -buf_s:]
    swa_v_sample = jnp.concatenate([cache_swa_v, kv_v_s.astype(cache_swa_v.dtype)], axis=1)[:, -buf_s:]
    return (y_prompt, y_sample, fox_k_prompt, fox_v_prompt, fox_logf_prompt, fox_k_sample,
            fox_v_sample, fox_logf_sample, swa_k_prompt, swa_v_prompt, swa_k_sample, swa_v_sample)
```

```python
import contextlib
import numpy as np
import concourse.bass as bass
import concourse.mybir as mybir
from concourse.bass_utils import run_bass_kernel_spmd

F32 = mybir.dt.float32
BF16 = mybir.dt.bfloat16
AF = mybir.ActivationFunctionType
ALU = mybir.AluOpType

ENGS = ['pe', 'act', 'dve', 'pool', 'sp']
NPOOL = 48
QPOOL = {'sp': (0, 28), 'act': (28, 12), 'pool': (40, 8)}

D = 1024
NH = 16
HD = 64
DFF = 2816
NM = 22
NS = 33
NQS = 34
NLBP = 64
LB_SC0 = 64
LB_S = 80
NBT = 81
EPS = 1e-6
NEG = -30000.0
TILES = [[0, 33]] + [[4 * i - 3, 4 * i - 2, 4 * i - 1, 4 * i] for i in range(1, 9)]


def lb_of(qs):
    return 31 + qs if qs < 33 else LB_S


class Op:
    __slots__ = ('eng', 'fn', 'dma', 'pos', 'need_inc', 'dma_id', 'clock', 'waits', 'incval', 'dbg')


class Sched:
    def __init__(self):
        self.ops = {e: [] for e in ENGS}
        self.tiles = {}
        self.dma_ops = {e: [] for e in ENGS}
        self.clock = {e: {} for e in ENGS}
        self.iv = {}
        self.alias = {}

    def region(self, name, lo, hi, group=None):
        self.iv[name] = (lo, hi, group)
        self.alias[name] = []
        for n2, (l2, h2, g2) in self.iv.items():
            if n2 != name and lo < h2 and l2 < hi and not (group is not None and g2 == group):
                self.alias[name].append(n2)
                self.alias[n2].append(name)

    def _expand(self, names):
        out = []
        for n in names:
            out.append(n)
            al = self.alias.get(n)
            if al:
                out.extend(al)
        return out

    def _known(self, eng, d):
        c = self.clock[eng]
        if d.dma:
            return c.get(('d', d.dma_id[0]), -1) >= d.dma_id[1]
        return c.get(('c', d.eng), -1) >= d.pos

    def op(self, eng, fn, reads=(), writes=(), dma=False):
        o = Op()
        o.eng = eng; o.fn = fn; o.dma = dma; o.need_inc = False
        o.pos = len(self.ops[eng])
        o.dma_id = None
        o.dbg = (tuple(reads), tuple(writes))
        reads0, writes0 = list(reads), list(writes)
        reads = self._expand(reads)
        writes = self._expand(writes)
        deps = []
        for t in reads:
            st = self.tiles.get(t)
            if st is not None and st[0] is not None:
                deps.append((st[0], 'raw'))
        for t in writes:
            st = self.tiles.get(t)
            if st is not None:
                if st[0] is not None:
                    deps.append((st[0], 'waw'))
                for r in st[1]:
                    deps.append((r, 'war'))
        if dma:
            base, size = QPOOL[eng]
            n = len(self.dma_ops[eng])
            o.dma_id = (base + n % size, n // size)
            if n >= size:
                deps.append((self.dma_ops[eng][n - size], 'guard'))
            self.dma_ops[eng].append(o)
        waits = []
        for d, kind in deps:
            if d is o:
                continue
            if (not d.dma) and (not dma) and d.eng == eng and eng == 'pe':
                continue
            if self._known(eng, d):
                continue
            waits.append(d)
            d.need_inc = True
            c = dict(self.clock[eng])
            for k, v in d.clock.items():
                if c.get(k, -1) < v:
                    c[k] = v
            k = ('d', d.dma_id[0]) if d.dma else ('c', d.eng)
            v = d.dma_id[1] if d.dma else d.pos
            if c.get(k, -1) < v:
                c[k] = v
            self.clock[eng] = c
        o.waits = waits
        o.clock = self.clock[eng]
        for t in reads0:
            st = self.tiles.get(t)
            if st is None:
                self.tiles[t] = [None, [o]]
            else:
                st[1].append(o)
        for t in writes0:
            self.tiles[t] = [o, []]
        self.ops[eng].append(o)
        return o

    def emit(self, nc):
        with contextlib.ExitStack() as es:
            csem = {e: es.enter_context(nc.semaphore('s_' + e)) for e in ENGS}
            dsem = [es.enter_context(nc.semaphore('d%d' % i)) for i in range(NPOOL)]
            for e in ENGS:
                c = 0
                for o in self.ops[e]:
                    if o.dma:
                        o.incval = 16 * (o.dma_id[1] + 1)
                    elif o.need_inc:
                        c += 1
                        o.incval = c
            block = es.enter_context(nc.Block())

            def run(engname, eng):
                for o in self.ops[engname]:
                    wmap = {}
                    for d in o.waits:
                        key = ('d', d.dma_id[0]) if d.dma else ('c', d.eng)
                        if key not in wmap or wmap[key] < d.incval:
                            wmap[key] = d.incval
                    for key, val in wmap.items():
                        sem = dsem[key[1]] if key[0] == 'd' else csem[key[1]]
                        eng.wait_ge(sem, val)
                    if o.fn is None:
                        continue
                    ins = o.fn(eng)
                    if o.dma:
                        ins.then_inc(dsem[o.dma_id[0]], 16)
                    elif o.need_inc:
                        ins.then_inc(csem[engname], 1)

            @block.tensor
            def _(pe):
                run('pe', pe)

            @block.scalar
            def _(act):
                run('act', act)

            @block.vector
            def _(dve):
                run('dve', dve)

            @block.gpsimd
            def _(pool):
                run('pool', pool)

            @block.sync
            def _(sp):
                run('sp', sp)


def build_program():
    nc = bass.Bass("TRN2", target_bir_lowering=False)
    S = Sched()

    def din(name, shape, dt=F32):
        return nc.dram_tensor(name, list(shape), dt, kind="ExternalInput").ap()

    def dout(name, shape):
        return nc.dram_tensor(name, list(shape), F32, kind="ExternalOutput").ap()

    def dscr(name, shape, dt=BF16):
        return nc.dram_tensor(name, list(shape), dt, kind="Internal").ap()

    xk = din("xk", [NLBP * 128, D])
    xs = din("xs", [128, D])
    pown = din("pown", [2, NQS * 128, 256])
    ck = din("ck", [2048, D]); cv = din("cv", [2048, D]); clf = din("clf", [2048, NH])
    cswk = din("cswk", [128, 256]); cswv = din("cswv", [128, 256])
    negvalid_d = din("negvalid", [NBT]); kill_d = din("kill", [NBT]); swk_d = din("swk", [NQS])
    ident_d = din("ident", [128, 128]); umat_d = din("umat", [128, 128]); sel0_d = din("sel0", [128, 128])
    masks_d = din("masks", [128, 65, 128])
    gpre_d = din("gpre", [128, 7, 8]); gpost_d = din("gpost", [4, D]); bgate_d = din("bgate", [2, D])
    bf_d = din("bf", [NH]); sinks_d = din("sinks", [NH])
    w_in = din("w_in", [D, 3088]); w_o0 = din("w_o0", [D, D]); w_q1 = din("w_q1", [D, D]); w_o1 = din("w_o1", [D, D])
    w_kv = din("w_kv", [D, 512])
    w_g = din("w_g", [2, D, DFF]); w_u = din("w_u", [2, D, DFF]); w_d = din("w_d", [2, DFF, D])
    w_pg = din("w_pg", [2, D, D]); w_pp = din("w_pp", [2, 256, D])
    y_o = dout("y", [NQS * 128, D]); fk_o = dout("fk", [NQS * 128, D]); fv_o = dout("fv", [NQS * 128, D])
    flf_o = dout("flf", [NQS * 128, NH]); kvo_o = dout("kvo", [NQS * 128, 512])
    swks_o = dout("swks", [128, 256]); swvs_o = dout("swvs", [128, 256])
    kscr = dscr("kscr", [NBT, 65, NH, 128]); vscr = dscr("vscr", [NBT, 128, NH, 65])
    wkvf_s = dscr("wkvf_s", [128, 8, 2064])
    wq_s = [dscr("wq_s%d" % l, [2, 128, 8, 512]) for l in range(2)]
    wo_s = [dscr("wo_s%d" % l, [2, 2, 64, 8, 512]) for l in range(2)]
    wg_s = [dscr("wg_s%d" % l, [11, 128, 2, 8, 128]) for l in range(2)]
    wu_s = [dscr("wu_s%d" % l, [11, 128, 2, 8, 128]) for l in range(2)]
    wd_s = [dscr("wd_s%d" % l, [2, 2, 128, 11, 512]) for l in range(2)]
    wpg_s = [dscr("wpg_s%d" % l, [2, 128, 8, 512]) for l in range(2)]
    wpp_s = [dscr("wpp_s%d" % l, [128, 2, D]) for l in range(2)]
    wkv_s = dscr("wkv_s", [128, 8, 512])

    es = contextlib.ExitStack()
    with es:
        def sb(name, shape, dt):
            return es.enter_context(nc.sbuf_tensor("sb_" + name, list(shape), dt))

        PS = es.enter_context(nc.psum_tensor("ps", [128, 8, 512], F32))
        identf = sb("identf", [128, 128], F32); identb = sb("identb", [128, 128], BF16)
        umat = sb("umat", [128, 128], F32); onesf = sb("onesf", [128, 128], F32); sel0 = sb("sel0", [128, 128], F32)
        masks = sb("masks", [128, 65, 128], BF16)
        gpre = sb("gpre", [128, 7, 8], F32)
        bfB = sb("bfB", [128, NH], F32)
        negvalid = sb("negvalidB", [128, NBT], F32); killB = sb("killB", [128, NBT], F32); swkB = sb("swkB", [128, NQS], F32)
        esink = sb("esink", [65, NH], F32)
        Fnk = sb("Fnk", [128, NBT, NH], F32); Fs = sb("Fs", [128, NBT, NH], F32); FrefB = sb("FrefB", [128, NBT, NH], F32)
        h = sb("h", [128, 4, D], F32)
        sst = sb("sst", [128, 40], F32); lnv = sb("lnv", [128, 4], F32); rstd = sb("rstd", [128, 4], F32)
        biasb = sb("biasb", [128, 32], F32)
        zt = sb("zt", [128, 3, NH], F32)
        KTr = sb("KTr", [128, 6, 4, 128], BF16); VAr = sb("VAr", [128, 7, 4, 65], BF16)
        WPB = 12288
        NARB = 3 * WPB + 92 * 1024
        AR = sb("arena", [128, NARB // 2], BF16)

        class Bump:
            def __init__(self, off):
                self.off = off

            def take(self, name, nbytes, shape, dt, parts=128, pattern=None, names=None, **kw):
                lo = self.off
                nbytes = (nbytes + 31) // 32 * 32
                self.off += nbytes
                assert self.off <= NARB, (name, self.off)
                v = AR[0:parts, lo // 2:(lo + nbytes) // 2]
                if dt == F32:
                    v = v.bitcast(F32)
                n = 1
                for s_ in shape[1:]:
                    n *= s_
                v = v[:, 0:n]
                if pattern:
                    v = v.rearrange(pattern, **kw)
                if names is None:
                    S.region(name, lo, lo + nbytes)
                else:
                    for nm in names:
                        S.region(nm, lo, lo + nbytes, group=name)
                return v

        WSL = 4096
        NWS = 3 * WPB // WSL
        for i in range(NWS):
            S.region('ws%d' % i, i * WSL, (i + 1) * WSL)
        S.region('wkvf', 0, 3 * WPB)
        wkvf = AR[:, 0:8 * 2064].rearrange("p (c n) -> p c n", n=2064)
        wp_ctr = [0]

        def wview(slot, shape, parts=128):
            n = 1
            for s_ in shape[1:]:
                n *= s_
            v = AR[0:parts, slot * WSL // 2: slot * WSL // 2 + n]
            if len(shape) == 3:
                v = v.rearrange("p (a b) -> p a b", b=shape[2])
            elif len(shape) == 4:
                v = v.rearrange("p (a b c) -> p a b c", b=shape[2], c=shape[3])
            return v

        B0 = 3 * WPB
        bp = Bump(B0)
        xt = [bp.take('xt%d' % i, 4096, [128, D], F32) for i in range(4)]
        hnp2 = [bp.take('hnp%d' % i, 2048, [128, D], BF16) for i in range(3)]
        hnp = hnp2[0]
        aTb2 = [bp.take('aTb%d' % i, 2048, [128, 8, 128], BF16, pattern="p (c t) -> p c t", t=128) for i in range(3)]
        kout = bp.take('kout', 4096, [128, D], F32)
        vout = bp.take('vout', 4096, [128, D], F32)
        ktm = [bp.take('ktm%d' % i, 2080, [128, NH, 65], BF16, pattern="p (h d) -> p h d", d=65) for i in range(2)]
        va = [bp.take('va%d' % i, 2080, [128, NH, 65], BF16, pattern="p (h d) -> p h d", d=65) for i in range(2)]
        kTsb = bp.take('kTsb', 4096, [65, NH, 128], BF16, parts=65, pattern="p (h t) -> p h t", t=128)
        LF = bp.take('LF', NBT * NH * 4, [128, NBT, NH], F32, pattern="p (b h) -> p b h", h=NH, names=[('LF', lb) for lb in range(NBT)])
        OFF = bp.take('OFF', NBT * NH * 4, [128, NBT, NH], F32, pattern="p (b h) -> p b h", h=NH)
        bt = Bump(B0)
        aT = bt.take('aTall', 8192, [128, 8, 512], BF16, pattern="p (c t) -> p c t", t=512, names=['aT%d' % j for j in range(4)])
        hn = [bt.take('hn%d' % i, 2048, [128, D], BF16) for i in range(2)]
        Qtm = bt.take('Qtm', 4 * NH * 65 * 2, [128, 4, NH, 65], BF16, pattern="p (j h d) -> p j h d", h=NH, d=65, names=[('Qtm', j) for j in range(4)])
        tmpA = bt.take('tmpA', 4096, [128, D], F32)
        tmpB = bt.take('tmpB', 4096, [128, D], F32)
        gbuf = [bt.take('gbuf%d' % i, 4096, [128, D], F32) for i in range(2)]
        ptile = bt.take('ptile', 4096, [128, 4, 256], F32, pattern="p (j n) -> p j n", n=256, names=[('ptile', j) for j in range(4)])
        pbf = bt.take('pbf', 2048, [128, 4, 256], BF16, pattern="p (j n) -> p j n", n=256, names=[('pbf', j) for j in range(4)])
        pTsb = bt.take('pTsb', 2048, [128, 2, 512], BF16, pattern="p (c t) -> p c t", t=512, names=[('pTsb', j) for j in range(4)])
        kvout = bt.take('kvout', 2048, [128, 512], F32)
        kvK = bt.take('kvK', 512, [128, 4, 64], BF16, pattern="p (g d) -> p g d", d=64)
        O65 = [bt.take('o65_%d' % i, 2048, [65, 512], F32, parts=65) for i in range(4)]
        sg = [bt.take('sg%d' % i, 1024, [128, 512], BF16) for i in range(2)]
        KTs = [bt.take('kts%d' % i, 1024, [128, 4, 128], BF16, pattern="p (h t) -> p h t", t=128) for i in range(6)]
        VAs = [bt.take('vas%d' % i, 672, [128, 336], BF16) for i in range(6)]
        attn_off = bt.off
        QTg = bt.take('QTg', 4096, [128, 4, 512], BF16, pattern="p (h t) -> p h t", t=512)
        oT = bt.take('oT', 16384, [128, NH, 512], BF16, pattern="p (h t) -> p h t", t=512, names=[('oT', i) for i in range(NH)])
        PT = [bt.take('pt%d' % i, 1024, [128, 512], BF16) for i in range(4)]
        bh = Bump(attn_off)
        hidT = bh.take('hidT', NM * 512 * 2, [128, NM, 512], BF16, pattern="p (m t) -> p m t", t=512, names=[('hidT', m) for m in range(NM)])
        aTn = ['aT%d' % j for j in range(4)]

        def OP(eng, fn, r=(), w=(), dma=False):
            return S.op(eng, fn, reads=list(r), writes=list(w), dma=dma)

        def MM(out, lhsT, rhs, st, sp_, r, w):
            OP('pe', lambda e, o=out, l=lhsT, rr=rhs: e.matmul(o, lhsT=l, rhs=rr, start=st, stop=sp_), r, w)

        def TR(out, in_, r, w):
            OP('pe', lambda e, o=out, i=in_: e.transpose(out=o, in_=i, identity=identb[:]), list(r) + ['identb'], w)

        def DMA(eng, out, in_, r, w):
            OP(eng, lambda e, o=out, i=in_: e.dma_start(out=o, in_=i), r, w, dma=True)

        def ACT(out, in_, func, r, w, **kw):
            OP('act', lambda e, o=out, i=in_: e.activation(out=o, in_=i, func=func, **kw), r, w)

        def TT(eng, out, in0, in1, op, r, w):
            OP(eng, lambda e, o=out, a=in0, b=in1: e.tensor_tensor(out=o, in0=a, in1=b, op=op), r, w)

        def TS(eng, out, in0, s1, s2, op0, op1, r, w):
            if op1 is None:
                OP(eng, lambda e, o=out, a=in0: e.tensor_scalar(out=o, in0=a, scalar1=s1, scalar2=None, op0=op0), r, w)
            else:
                OP(eng, lambda e, o=out, a=in0: e.tensor_scalar(out=o, in0=a, scalar1=s1, scalar2=s2, op0=op0, op1=op1), r, w)

        def CP(eng, out, in_, r, w):
            OP(eng, lambda e, o=out, i=in_: e.tensor_copy(out=o, in_=i), r, w)

        def MS(eng, ap, val, w):
            OP(eng, lambda e, a=ap: e.memset(a, val), (), w)

        def psb(b):
            return 'ps%d' % b

        def ps_bf(b0, nb, parts, pattern, **kw):
            v = PS[0:parts, b0:b0 + nb, :].rearrange("p a b -> p (a b)").bitcast(BF16)
            return v.rearrange(pattern, **kw)

        allreg = list(S.iv.keys())
        MS('pool', AR[:, 0:NARB // 4], 0.0, allreg)
        MS('dve', AR[:, NARB // 4:NARB // 2], 0.0, allreg)
        DMA('sp', identf[:], ident_d, [], ['identf'])
        DMA('sp', umat[:], umat_d, [], ['umat'])
        DMA('sp', sel0[:], sel0_d, [], ['sel0'])
        DMA('sp', gpre[:], gpre_d, [], ['gpre'])
        DMA('sp', bfB[:], bf_d.partition_broadcast(128), [], ['bfB'])
        DMA('sp', negvalid[:], negvalid_d.partition_broadcast(128), [], ['negvalid'])
        DMA('sp', killB[:], kill_d.partition_broadcast(128), [], ['killB'])
        DMA('sp', swkB[:], swk_d.partition_broadcast(128), [], ['swkB'])
        DMA('sp', esink[64:65, :], sinks_d.rearrange("(o n) -> o n", o=1), [], ['esink'])
        DMA('pool', masks[:], masks_d, [], ['masks'])
        CP('dve', identb[:], identf[:], ['identf'], ['identb'])
        MS('pool', onesf[:], 1.0, ['onesf'])
        MS('pool', KTr[:], 0.0, ['KTr0', 'KTr1', 'KTr2', 'KTr3', 'KTr4', 'KTr5'])
        MS('pool', VAr[:], 1.0, ['VAr0', 'VAr1', 'VAr2', 'VAr3', 'VAr4', 'VAr5'])
        VArf = VAr[:].rearrange("p b g d -> p (b g d)")
        MS('pool', VAr[:, 0, :, 0:64], 0.0, ['VAr0'])
        for i in range(2):
            MS('pool', ktm[i][:, :, 64:65], 1.0, ['ktm%d' % i])
            MS('pool', va[i][:, :, 64:65], 1.0, ['va%d' % i])
        ACT(esink[64:65, :], esink[64:65, :], AF.Exp, ['esink'], ['esink'])

        def conv(dst, src, name):
            DMA('pool', dst, src, [], [name])

        for c in range(8):
            conv(wkvf_s[:, c, :], w_in[c * 128:(c + 1) * 128, 1024:3088], ('wkvf_s', c))
        for hf in range(2):
            conv(wq_s[0][hf], w_in[:, hf * 512:(hf + 1) * 512].rearrange("(c p) n -> p c n", p=128), ('wq_s', 0, hf))

        def conv_wo(l, src):
            for nh in range(2):
                for hh in range(2):
                    conv(wo_s[l][nh, hh], src[hh * 512:(hh + 1) * 512, nh * 512:(nh + 1) * 512].rearrange("(h d) n -> d h n", d=64), ('wo_s', l, nh, hh))

        def conv_ffn(l):
            for gi in range(11):
                for mm_ in range(2):
                    m = gi * 2 + mm_
                    conv(wg_s[l][gi, :, mm_], w_g[l, :, m * 128:(m + 1) * 128].rearrange("(c p) n -> p c n", p=128), ('wg_s', l, gi, mm_))
                    conv(wu_s[l][gi, :, mm_], w_u[l, :, m * 128:(m + 1) * 128].rearrange("(c p) n -> p c n", p=128), ('wu_s', l, gi, mm_))
            for nh in range(2):
                for mh in range(2):
                    conv(wd_s[l][nh, mh], w_d[l, mh * 1408:(mh + 1) * 1408, nh * 512:(nh + 1) * 512].rearrange("(m p) n -> p m n", p=128), ('wd_s', l, nh, mh))

        def conv_ple(l):
            for nh in range(2):
                conv(wpg_s[l][nh], w_pg[l, :, nh * 512:(nh + 1) * 512].rearrange("(c p) n -> p c n", p=128), ('wpg_s', l, nh))
            conv(wpp_s[l], w_pp[l].rearrange("(c p) n -> p c n", p=128), ('wpp_s', l))

        conv_wo(0, w_o0)
        conv_ffn(0)
        conv_ple(0)
        conv(wkv_s, w_kv.rearrange("(c p) n -> p c n", p=128), 'wkv_s')
        for hf in range(2):
            conv(wq_s[1][hf], w_q1[:, hf * 512:(hf + 1) * 512].rearrange("(c p) n -> p c n", p=128), ('wq_s', 1, hf))
        conv_wo(1, w_o1)
        conv_ffn(1)
        conv_ple(1)

        DMA('sp', wkvf, wkvf_s, [('wkvf_s', c) for c in range(8)], ['wkvf'])
        psT0 = ps_bf(0, 1, 128, "p (c t) -> p c t", t=128)
        pskT = ps_bf(6, 2, 65, "p (h t) -> p h t", t=128)

        def kv_to_scratch(lb, i, own_qs):
            DMA('act', vscr[lb], va[i], ['va%d' % i], [('vscr', lb)])
            for hh in range(NH):
                TR(pskT[:, hh, :], ktm[i][:, hh, :], ['ktm%d' % i], [psb(6), psb(7)])
            CP('dve', kTsb, pskT, [psb(6), psb(7)], ['kTsb'])
            DMA('act', kscr[lb], kTsb, ['kTsb'], [('kscr', lb)])

        xblocks = [(lb, xk[lb * 128:(lb + 1) * 128, :], (lb - 31) if lb >= 31 else None) for lb in range(NLBP)]
        xblocks.append((LB_S, xs, 33))
        def stageA1(bi):
            lb, src, own_qs = xblocks[bi]
            x4 = bi % 4
            hb = hnp2[bi % 3]
            hname = 'hnp%d' % (bi % 3)
            sn = 'sstp%d' % x4
            DMA('sp', xt[x4], src, [], ['xt%d' % x4])
            MS('pool', sst[:, 32 + x4:33 + x4], 0.0, [sn])
            ACT(hb, xt[x4], AF.Square, ['xt%d' % x4], [hname, sn], scale=1.0 / 32, accum_out=sst[:, 32 + x4:33 + x4])
            TS('dve', sst[:, 32 + x4:33 + x4], sst[:, 32 + x4:33 + x4], EPS, None, ALU.add, None, [sn], [sn])
            ACT(sst[:, 36 + x4:37 + x4], sst[:, 32 + x4:33 + x4], AF.Ln, [sn], [sn + 'l'])
            ACT(sst[:, 36 + x4:37 + x4], sst[:, 36 + x4:37 + x4], AF.Exp, [sn + 'l'], [sn + 'l'], scale=-0.5)
            ACT(hb, xt[x4], AF.Copy, ['xt%d' % x4, sn + 'l'], [hname], scale=sst[:, 36 + x4:37 + x4])

        def stageA2(bi):
            hb = hnp2[bi % 3]
            hname = 'hnp%d' % (bi % 3)
            x3 = bi % 3
            for c in range(8):
                TR(psT0[:, c, :], hb[:, c * 128:(c + 1) * 128], [hname], [psb(0)])
            TT('dve', aTb2[x3], psT0, gpre[:, 0, :].unsqueeze(2).to_broadcast([128, 8, 128]), ALU.mult, [psb(0), 'gpre'], ['aTb%d' % x3])

        def stageB(bi):
            lb, src, own_qs = xblocks[bi]
            i = bi % 2
            aTb = aTb2[bi % 3]
            an = 'aTb%d' % (bi % 3)
            for (bk, c0, c1) in ((1, 0, 512), (2, 512, 1024), (3, 1024, 1536), (4, 1536, 2048), (5, 2048, 2064)):
                for c in range(8):
                    MM(PS[:, bk, 0:c1 - c0], aTb[:, c, :], wkvf[:, c, c0:c1], c == 0, c == 7, [an, 'wkvf'], [psb(bk)])
            psk = PS[:, 1:3, :]
            psv = PS[:, 3:5, :]
            CP('dve', ktm[i][:, :, 0:64], psk.rearrange("p a (h d) -> p (a h) d", d=64), [psb(1), psb(2)], ['ktm%d' % i])
            CP('dve', va[i][:, :, 0:64], psv.rearrange("p a (h d) -> p (a h) d", d=64), [psb(3), psb(4)], ['va%d' % i])
            if own_qs is not None:
                ACT(kout.rearrange("p (a b) -> p a b", b=512), psk, AF.Copy, [psb(1), psb(2)], ['kout', psb(1), psb(2)])
                DMA('act', fk_o[own_qs * 128:(own_qs + 1) * 128, :], kout, ['kout'], [('fk', own_qs)])
                ACT(vout.rearrange("p (a b) -> p a b", b=512), psv, AF.Copy, [psb(3), psb(4)], ['vout', psb(3), psb(4)])
                DMA('act', fv_o[own_qs * 128:(own_qs + 1) * 128, :], vout, ['vout'], [('fv', own_qs)])
            TT('dve', zt[:, 0, :], PS[:, 5, 0:NH], bfB[:], ALU.add, [psb(5), 'bfB'], ['zt0'])
            ACT(zt[:, 1, :], zt[:, 0, :], AF.Exp, ['zt0'], ['zt1'], scale=-1.0)
            TS('dve', zt[:, 1, :], zt[:, 1, :], 1.0, None, ALU.add, None, ['zt1'], ['zt1'])
            ACT(zt[:, 2, :], zt[:, 1, :], AF.Ln, ['zt1'], ['zt2'])
            TS('dve', LF[:, lb, :], zt[:, 2, :], negvalid[:, lb:lb + 1], None, ALU.mult, None, ['zt2', 'negvalid'], [('LF', lb)])
            if own_qs is not None:
                DMA('act', flf_o[own_qs * 128:(own_qs + 1) * 128, :], LF[:, lb, :], [('LF', lb)], [('flf', own_qs)])
            kv_to_scratch(lb, i, own_qs)

        nblk = len(xblocks)
        stageA1(0); stageA1(1); stageA1(2)
        stageA2(0); stageA2(1)
        for bi in range(nblk):
            if bi + 3 < nblk:
                stageA1(bi + 3)
            if bi + 2 < nblk:
                stageA2(bi + 2)
            stageB(bi)
        for cb in range(16):
            lb = LB_SC0 + cb
            i = cb % 2
            DMA('sp', kout, ck[cb * 128:(cb + 1) * 128, :], [], ['kout'])
            DMA('sp', vout, cv[cb * 128:(cb + 1) * 128, :], [], ['vout'])
            DMA('sp', LF[:, lb, :], clf[cb * 128:(cb + 1) * 128, :], [], [('LF', lb)])
            CP('dve', ktm[i][:, :, 0:64], kout.rearrange("p (h d) -> p h d", d=64), ['kout'], ['ktm%d' % i])
            CP('dve', va[i][:, :, 0:64], vout.rearrange("p (h d) -> p h d", d=64), ['vout'], ['va%d' % i])
            kv_to_scratch(lb, i, None)
        DMA('sp', kout[:, 0:256], cswk, [], ['kout'])
        DMA('sp', vout[:, 0:256], cswv, [], ['vout'])
        DMA('sp', swks_o[0:112, :], cswk[16:128, :], [], [('swks', 0)])
        DMA('sp', swvs_o[0:112, :], cswv[16:128, :], [], [('swvs', 0)])
        CP('dve', hnp[:, 0:256], kout[:, 0:256], ['kout'], ['hnp0'])
        CP('dve', VAr[:, 5, :, 0:64], vout[:, 0:256].rearrange("p (g d) -> p g d", d=64), ['vout'], ['VAr5'])
        psk4 = ps_bf(0, 1, 64, "p (g t) -> p g t", t=128)
        for g in range(4):
            TR(psk4[:, g, :], hnp[:, g * 64:(g + 1) * 64], ['hnp0'], [psb(0)])
        CP('dve', KTr[0:64, 5, :, :], psk4[:, 0:4, :], [psb(0)], ['KTr5'])

        allLF = [('LF', lb) for lb in range(NBT)]
        LFf = LF.rearrange("p b h -> p (b h)")
        NCOL = NBT * NH
        chunks = [(0, 512), (512, 1024), (1024, NCOL)]
        for ci, (c0, c1) in enumerate(chunks):
            MM(PS[:, ci, 0:c1 - c0], umat[:], LFf[:, c0:c1], True, True, allLF + ['umat'], [psb(ci)])
            MM(PS[:, 3 + ci, 0:c1 - c0], onesf[:], LFf[:, c0:c1], True, True, allLF + ['onesf'], [psb(3 + ci)])
        MS('dve', OFF[:, 0, :], 0.0, ['OFF'])
        MS('dve', OFF[:, LB_SC0, :], 0.0, ['OFF'])
        for b in range(1, NBT):
            if b == LB_SC0:
                continue
            pb = b - 1
            bank, col = 3 + (pb * NH) // 512, (pb * NH) % 512
            TT('dve', OFF[:, b, :], OFF[:, pb, :], PS[:, bank, col:col + NH], ALU.add, ['OFF', psb(bank)], ['OFF'])
        OFFf = OFF.rearrange("p b h -> p (b h)")
        Fnkf = Fnk[:].rearrange("p b h -> p (b h)")
        for ci, (c0, c1) in enumerate(chunks):
            TT('dve', Fnkf[:, c0:c1], PS[:, ci, 0:c1 - c0], OFFf[:, c0:c1], ALU.add, [psb(ci), 'OFF'], ['Fnk'])
        TT('dve', Fs[:], Fnk[:], killB[:].unsqueeze(2).to_broadcast([128, NBT, NH]), ALU.add, ['Fnk', 'killB'], ['Fs'])
        for ci, (c0, c1) in enumerate(chunks):
            MM(PS[:, 6, 0:c1 - c0], sel0[:], Fnkf[:, c0:c1], True, True, ['Fnk', 'sel0'], [psb(6)])
            CP('dve', FrefB[:].rearrange("p b h -> p (b h)")[:, c0:c1], PS[:, 6, 0:c1 - c0], [psb(6)], ['FrefB'])

        def wload(src, shape, srcname, parts=128):
            n = 2
            for s_ in shape[1:]:
                n *= s_
            ns = (n + WSL - 1) // WSL
            if wp_ctr[0] + ns > NWS:
                wp_ctr[0] = 0
            s0 = wp_ctr[0]
            wp_ctr[0] += ns
            v = wview(s0, shape, parts)
            names = ['ws%d' % i for i in range(s0, s0 + ns)]
            DMA('sp', v, src, srcname if isinstance(srcname, list) else [srcname], names)
            return v, names

        gb_ctr = [0]

        def gload(src_row):
            i = gb_ctr[0] % 2
            gb_ctr[0] += 1
            DMA('sp', gbuf[i], src_row.partition_broadcast(128), [], ['gbuf%d' % i])
            return gbuf[i], 'gbuf%d' % i

        def norm_pre(nt, gi, slotwise=False, after_slot=None):
            if slotwise:
                for j in range(nt):
                    n1, n2 = ('nsq', j), ('nrs', j)
                    MS('pool', sst[:, j:j + 1], 0.0, [n1])
                    ACT(hn[j % 2], h[:, j, :], AF.Square, [('h', j)], ['hn%d' % (j % 2), n1], scale=1.0 / 32, accum_out=sst[:, j:j + 1])
                    TS('dve', sst[:, j:j + 1], sst[:, j:j + 1], EPS, None, ALU.add, None, [n1], [n1])
                    ACT(lnv[:, j:j + 1], sst[:, j:j + 1], AF.Ln, [n1], [n2])
                    ACT(rstd[:, j:j + 1], lnv[:, j:j + 1], AF.Exp, [n2], [n2], scale=-0.5)
                    ACT(hn[j % 2], h[:, j, :], AF.Copy, [('h', j), n2], ['hn%d' % (j % 2)], scale=rstd[:, j:j + 1])
                    bk = j % 2
                    pv = ps_bf(bk, 1, 128, "p (c t) -> p c t", t=128)
                    for c in range(8):
                        TR(pv[:, c, :], hn[j % 2][:, c * 128:(c + 1) * 128], ['hn%d' % (j % 2)], [psb(bk)])
                    TT('dve', aT[:, :, j * 128:(j + 1) * 128], pv, gpre[:, gi, :].unsqueeze(2).to_broadcast([128, 8, 128]), ALU.mult,
                       [psb(bk), 'gpre'], [aTn[j]])
                    if after_slot is not None:
                        after_slot(j)
                return
            names = [('nsq', j) for j in range(nt)]
            rnames = [('nrs', j) for j in range(nt)]
            MS('pool', sst[:, 0:nt], 0.0, names)
            for j in range(nt):
                ACT(hn[j % 2], h[:, j, :], AF.Square, [('h', j)], ['hn%d' % (j % 2), ('nsq', j)], scale=1.0 / 32, accum_out=sst[:, j:j + 1])
            TS('dve', sst[:, 0:nt], sst[:, 0:nt], EPS, None, ALU.add, None, names, names)
            ACT(lnv[:, 0:nt], sst[:, 0:nt], AF.Ln, names, rnames)
            ACT(rstd[:, 0:nt], lnv[:, 0:nt], AF.Exp, rnames, rnames, scale=-0.5)
            for j in range(nt):
                if j % 2 == 0:
                    ACT(hn[j % 2], h[:, j, :], AF.Copy, [('h', j), ('nrs', j)], ['hn%d' % (j % 2)], scale=rstd[:, j:j + 1])
                else:
                    TS('dve', hn[j % 2], h[:, j, :], rstd[:, j:j + 1], None, ALU.mult, None, [('h', j), ('nrs', j)], ['hn%d' % (j % 2)])
                bk = j % 2
                pv = ps_bf(bk, 1, 128, "p (c t) -> p c t", t=128)
                for c in range(8):
                    TR(pv[:, c, :], hn[j % 2][:, c * 128:(c + 1) * 128], ['hn%d' % (j % 2)], [psb(bk)])
                TT('dve', aT[:, :, j * 128:(j + 1) * 128], pv, gpre[:, gi, :].unsqueeze(2).to_broadcast([128, 8, 128]), ALU.mult,
                   [psb(bk), 'gpre'], [aTn[j]])

        def post_norm_all(nt, bankfn, gB, gname):
            MS('pool', sst[:, 8:16], 0.0, [('pn', j) for j in range(4)])
            tmps = (tmpA, tmpB)
            tnames = ('tmpA', 'tmpB')
            for j in range(nt):
                b0, b1 = bankfn(j)
                jn = hn[j % 2]
                ACT(jn[:, 0:512], PS[:, b0, :], AF.Square, [psb(b0)], ['hn%d' % (j % 2), ('pn', j), psb(b0)], scale=1.0 / 32, accum_out=sst[:, 8 + 2 * j:9 + 2 * j])
                ACT(jn[:, 512:1024], PS[:, b1, :], AF.Square, [psb(b1)], ['hn%d' % (j % 2), ('pn', j), psb(b1)], scale=1.0 / 32, accum_out=sst[:, 9 + 2 * j:10 + 2 * j])
                t = tmps[j % 2]
                tn = tnames[j % 2]
                for hf, bk in enumerate((b0, b1)):
                    TT('dve', t[:, hf * 512:(hf + 1) * 512], PS[:, bk, :], gB[:, hf * 512:(hf + 1) * 512], ALU.mult, [psb(bk), gname], [tn])
                TS('dve', sst[:, 16 + j:17 + j], sst[:, 8 + 2 * j:9 + 2 * j], sst[:, 9 + 2 * j:10 + 2 * j], EPS, ALU.add, ALU.add, [('pn', j)], [('pn2', j)])
                ACT(sst[:, 20 + j:21 + j], sst[:, 16 + j:17 + j], AF.Ln, [('pn2', j)], [('pn3', j)])
                ACT(sst[:, 24 + j:25 + j], sst[:, 20 + j:21 + j], AF.Exp, [('pn3', j)], [('pn4', j)], scale=-0.5)
                OP('dve', lambda e, j=j, t=t: e.scalar_tensor_tensor(out=h[:, j, :], in0=t, scalar=sst[:, 24 + j:25 + j], in1=h[:, j, :],
                                                                      op0=ALU.mult, op1=ALU.add),
                   [tn, ('pn4', j), ('h', j)], [('h', j)])

        kv_ctr = [0]; bias_ctr = [0]; pt_ctr = [0]; sbk_ctr = [0]; rl_ctr = [0]

        pending = []

        def normalize_heads(hg, gc0, gw, sink=False):
            for hh in range(4):
                ob = 4 + hh
                CP('dve', O65[hh][:, 0:gw], PS[0:65, ob, gc0:gc0 + gw], [psb(ob)], ['o65_%d' % hh])

            def part2():
                for hh in range(4):
                    hd_ = hg * 4 + hh
                    on = 'o65_%d' % hh
                    if sink:
                        TS('dve', O65[hh][64:65, 0:gw], O65[hh][64:65, 0:gw], esink[64:65, hd_:hd_ + 1], None, ALU.add, None, [on, 'esink'], [on])
                    OP('dve', lambda e, hh=hh: e.reciprocal(out=O65[hh][64:65, 0:gw], in_=O65[hh][64:65, 0:gw]), [on], [on])
                for hh in range(4):
                    hd_ = hg * 4 + hh
                    on = 'o65_%d' % hh
                    sbk = SB_BANKS[sbk_ctr[0] % 3]
                    sbk_ctr[0] += 1
                    MM(PS[0:64, sbk, 0:gw], onesf[64:65, 0:64], O65[hh][64:65, 0:gw], True, True, ['onesf', on], [psb(sbk)])
                    TT('dve', oT[0:64, hd_, gc0:gc0 + gw], O65[hh][0:64, 0:gw], PS[0:64, sbk, 0:gw], ALU.mult, [on, psb(sbk)], [('oT', hd_)])
            pending.append(part2)

        def flush_pending():
            while pending:
                pending.pop(0)()

        def q_transposes(hg, poss, rows):
            c0, c1 = poss[0] * 128, (poss[-1] + 1) * 128
            for half in range(2):
                psq = ps_bf(0, 1, 65, "p (h t) -> p h t", t=512)
                for j in poss:
                    for h2 in range(2):
                        hh = half * 2 + h2
                        TR(psq[0:rows, h2, j * 128:(j + 1) * 128], Qtm[:, j, hg * 4 + hh, 0:rows], [('Qtm', j)], [psb(0)])
                CP('dve', QTg[0:rows, half * 2:half * 2 + 2, c0:c1], psq[0:rows, :, c0:c1], [psb(0)], ['QTg'])

        SB_BANKS = [1, 2, 3]
        LAG = 2

        def fox_attn(poss, lbref, pre_lbs, in_lbs):
            gc0 = poss[0] * 128
            gw = len(poss) * 128
            steps = [(lb, 0, False) for lb in pre_lbs] + [(lb, jj * 128, True) for jj, lb in enumerate(in_lbs)]
            for hg in range(4):
                q_transposes(hg, poss, 65)
                items = [(si, hh) for si in range(len(steps)) for hh in range(4)]
                st = {}
                defer_at = min(32, len(items) // 2)
                for idx in range(len(items) + LAG):
                    if idx == defer_at:
                        flush_pending()
                    if idx < len(items):
                        si, hh = items[idx]
                        lb, q0, diag = steps[si]
                        if hh == 0:
                            kb = kv_ctr[0] % 6
                            kv_ctr[0] += 1
                            DMA('sp', KTs[kb][0:65], kscr[lb, :, hg * 4:(hg + 1) * 4, :], [('kscr', lb)], ['kts%d' % kb])
                            DMA('sp', VAs[kb][:, 0:260].rearrange("p (h d) -> p h d", d=65), vscr[lb, :, hg * 4:(hg + 1) * 4, :], [('vscr', lb)], ['vas%d' % kb])
                            bb = bias_ctr[0] % 8
                            bias_ctr[0] += 1
                            Ft, Fn = (Fnk, 'Fnk') if diag else (Fs, 'Fs')
                            TT('dve', biasb[:, bb * 4:(bb + 1) * 4], FrefB[:, lbref, hg * 4:(hg + 1) * 4], Ft[:, lb, hg * 4:(hg + 1) * 4], ALU.subtract,
                               ['FrefB', Fn], ['bias%d' % bb])
                            st[si] = (kb, bb)
                        kb, bb = st[si]
                        ncols = gw - q0
                        sbk = SB_BANKS[sbk_ctr[0] % 3]
                        sbk_ctr[0] += 1
                        MM(PS[:, sbk, 0:ncols], KTs[kb][:, hh, :], QTg[:, hh, gc0 + q0:gc0 + gw], True, not diag, ['kts%d' % kb, 'QTg'], [psb(sbk)])
                        if diag:
                            MM(PS[:, sbk, 0:128], identb[:], masks[:, 0, :], False, True, ['identb', 'masks'], [psb(sbk)])
                        pb = pt_ctr[0] % 4
                        pt_ctr[0] += 1
                        ACT(PT[pb][:, 0:ncols], PS[:, sbk, 0:ncols], AF.Exp, [psb(sbk), 'bias%d' % bb], ['pt%d' % pb],
                            bias=biasb[:, bb * 4 + hh:bb * 4 + hh + 1], scale=1.0)
                        st[(si, hh)] = pb
                    if idx >= LAG:
                        si, hh = items[idx - LAG]
                        lb, q0, diag = steps[si]
                        kb, bb = st[si]
                        pb = st[(si, hh)]
                        ncols = gw - q0
                        MM(PS[:, 4 + hh, gc0 + q0:gc0 + gw], VAs[kb][:, hh * 65:hh * 65 + 128], PT[pb][:, 0:ncols], si == 0, si == len(steps) - 1,
                           ['vas%d' % kb, 'pt%d' % pb], [psb(4 + hh)])
                normalize_heads(hg, gc0, gw)

        def swa_attn(tile_qs):
            nt = len(tile_qs)
            for hg in range(4):
                q_transposes(hg, list(range(nt)), 64)
                st = {}
                for idx in range(nt + 1):
                    if idx == 1:
                        flush_pending()
                    if idx < nt:
                        j = idx
                        qs = tile_qs[j]
                        if qs == 33:
                            pbuf, mprev, mcur = 5, 33 + hg * 4, 49 + hg * 4
                        else:
                            pbuf, mprev, mcur = (0 if j == 0 else j), 1 + hg * 4, 17 + hg * 4
                        cbuf = j + 1
                        qv = QTg[:, :, j * 128:(j + 1) * 128]
                        sb1 = SB_BANKS[sbk_ctr[0] % 3]
                        sb2 = SB_BANKS[(sbk_ctr[0] + 1) % 3]
                        sbk_ctr[0] += 2
                        MM(PS[:, sb1, :], KTr[:, pbuf, hg, :], qv, True, False, ['KTr%d' % pbuf, 'QTg'], [psb(sb1)])
                        MM(PS[:, sb1, :], identb[:], masks[:, mprev:mprev + 4, :], False, True, ['identb', 'masks'], [psb(sb1)])
                        MM(PS[:, sb2, :], KTr[:, cbuf, hg, :], qv, True, False, ['KTr%d' % cbuf, 'QTg'], [psb(sb2)])
                        MM(PS[:, sb2, :], identb[:], masks[:, mcur:mcur + 4, :], False, True, ['identb', 'masks'], [psb(sb2)])
                        p1 = pt_ctr[0] % 4
                        p2 = (pt_ctr[0] + 1) % 4
                        pt_ctr[0] += 2
                        ACT(PT[p1], PS[:, sb1, :], AF.Exp, [psb(sb1), 'swkB'], ['pt%d' % p1], bias=swkB[:, qs:qs + 1], scale=1.0)
                        ACT(PT[p2], PS[:, sb2, :], AF.Exp, [psb(sb2)], ['pt%d' % p2])
                        st[j] = (p1, p2, pbuf, cbuf)
                    if idx >= 1:
                        j = idx - 1
                        p1, p2, pbuf, cbuf = st[j]
                        MM(PS[:, 4 + j, :], VArf[:, (pbuf * 4 + hg) * 65:(pbuf * 4 + hg) * 65 + 128], PT[p1], True, False, ['VAr%d' % pbuf, 'pt%d' % p1], [psb(4 + j)])
                        MM(PS[:, 4 + j, :], VArf[:, (cbuf * 4 + hg) * 65:(cbuf * 4 + hg) * 65 + 128], PT[p2], False, True, ['VAr%d' % cbuf, 'pt%d' % p2], [psb(4 + j)])
                for j in range(nt):
                    CP('dve', O65[j], PS[0:65, 4 + j, :], [psb(4 + j)], ['o65_%d' % j])

                def part2(hg=hg):
                    for j in range(nt):
                        on = 'o65_%d' % j
                        r3 = O65[j][64:65, :].rearrange("p (h q) -> p h q", q=128)
                        TT('dve', r3, r3, esink[64:65, hg * 4:(hg + 1) * 4].unsqueeze(2).to_broadcast([1, 4, 128]), ALU.add, [on, 'esink'], [on])
                        ACT(O65[j][64:65, :], O65[j][64:65, :], AF.Ln, [on], [on])
                        ACT(O65[j][64:65, :], O65[j][64:65, :], AF.Exp, [on], [on], scale=-1.0)
                    for j in range(nt):
                        on = 'o65_%d' % j
                        sbk = SB_BANKS[sbk_ctr[0] % 3]
                        sbk_ctr[0] += 1
                        MM(PS[0:64, sbk, :], onesf[64:65, 0:64], O65[j][64:65, :], True, True, ['onesf', on], [psb(sbk)])
                        TT('dve', oT[0:64, hg * 4:(hg + 1) * 4, j * 128:(j + 1) * 128], O65[j][0:64, :].rearrange("p (h q) -> p h q", q=128),
                           PS[0:64, sbk, :].rearrange("p (h q) -> p h q", q=128), ALU.mult, [on, psb(sbk)], [('oT', hg * 4 + i) for i in range(4)])
                pending.append(part2)

        def q_proj(l, nt, tile_qs, gi):
            Ws = [wload(wq_s[l][hf], [128, 8, 512], ('wq_s', l, hf)) for hf in range(2)]
            rot = [0]

            def per_slot(j):
                for hf in range(2):
                    W, wn = Ws[hf]
                    bk = 2 + rot[0] % 4
                    rot[0] += 1
                    for c in range(8):
                        MM(PS[:, bk, :], aT[:, c, j * 128:(j + 1) * 128], W[:, c, :], c == 0, c == 7, [aTn[j]] + wn, [psb(bk)])
                    ACT(Qtm[:, j, hf * 8:(hf + 1) * 8, 0:64], PS[:, bk, :].rearrange("p (h d) -> p h d", d=64), AF.Copy, [psb(bk)], [('Qtm', j)], scale=0.125)
            norm_pre(nt, gi, slotwise=True, after_slot=per_slot)

        def out_proj(l, nt, gidx):
            flush_pending()
            for nh in range(2):
                for hh2 in range(2):
                    W, wn = wload(wo_s[l][nh, hh2], [64, 8, 512], ('wo_s', l, nh, hh2), parts=64)
                    W = wview(int(wn[0][2:]), [128, 8, 512])
                    for j in range(nt):
                        bk = nh * 4 + j
                        for i in range(8):
                            hd_ = hh2 * 8 + i
                            MM(PS[:, bk, :], oT[:, hd_, j * 128:(j + 1) * 128], W[:, i, :], hh2 == 0 and i == 0, hh2 == 1 and i == 7,
                               [('oT', hd_)] + wn, [psb(bk)])
            gB, gname = gload(gpost_d[gidx])
            post_norm_all(nt, lambda j: (j, 4 + j), gB, gname)

        def ffn(l, nt, gi_pre, gidx_post):
            ntk = nt * 128
            norm_pre(nt, gi_pre)
            for gi in range(11):
                Wg, wgn = wload(wg_s[l][gi], [128, 2, 8, 128], [('wg_s', l, gi, 0), ('wg_s', l, gi, 1)])
                Wu, wun = wload(wu_s[l][gi], [128, 2, 8, 128], [('wu_s', l, gi, 0), ('wu_s', l, gi, 1)])
                for mm_ in range(2):
                    m = gi * 2 + mm_
                    bg, bu = 2 * (m % 2), 2 * (m % 2) + 1
                    for c in range(8):
                        MM(PS[:, bg, 0:ntk], Wg[:, mm_, c, :], aT[:, c, 0:ntk], c == 0, c == 7, wgn + aTn[:nt], [psb(bg)])
                    for c in range(8):
                        MM(PS[:, bu, 0:ntk], Wu[:, mm_, c, :], aT[:, c, 0:ntk], c == 0, c == 7, wun + aTn[:nt], [psb(bu)])
                    ACT(sg[m % 2][:, 0:ntk], PS[:, bg, 0:ntk], AF.Silu, [psb(bg)], ['sg%d' % (m % 2)])
                    TT('dve', hidT[:, m, 0:ntk], sg[m % 2][:, 0:ntk], PS[:, bu, 0:ntk], ALU.mult, ['sg%d' % (m % 2), psb(bu)], [('hidT', m)])
            for nh in range(2):
                for mh in range(2):
                    W, wn = wload(wd_s[l][nh, mh], [128, 11, 512], ('wd_s', l, nh, mh))
                    for j in range(nt):
                        bk = (4 + j) if nh == 0 else j
                        for i in range(11):
                            m = mh * 11 + i
                            MM(PS[:, bk, :], hidT[:, m, j * 128:(j + 1) * 128], W[:, i, :], mh == 0 and i == 0, mh == 1 and i == 10,
                               [('hidT', m)] + wn, [psb(bk)])
            gB, gname = gload(gpost_d[gidx_post])
            post_norm_all(nt, lambda j: (4 + j, j), gB, gname)

        def ple(l, nt, tile_qs, gi_pre):
            for j, qs in enumerate(tile_qs):
                DMA('sp', ptile[:, j, :], pown[l, qs * 128:(qs + 1) * 128, :], [], [('ptile', j)])
                CP('dve', pbf[:, j, :], ptile[:, j, :], [('ptile', j)], [('pbf', j)])
                pv = ps_bf(7, 1, 128, "p (c t) -> p c t", t=128)
                for c2 in range(2):
                    TR(pv[:, c2, :], pbf[:, j, c2 * 128:(c2 + 1) * 128], [('pbf', j)], [psb(7)])
                CP('dve', pTsb[:, :, j * 128:(j + 1) * 128], pv[:, 0:2, :], [psb(7)], [('pTsb', j)])
            Wg0, n0 = wload(wpg_s[l][0], [128, 8, 512], ('wpg_s', l, 0))
            Wg1, n1 = wload(wpg_s[l][1], [128, 8, 512], ('wpg_s', l, 1))
            Wp, n2 = wload(wpp_s[l], [128, 2, D], ('wpp_s', l))
            bB, bname = gload(bgate_d[l])

            def per_slot(j):
                b0 = 2 + (j % 2) * 3
                bg0, bg1, bp0 = b0, b0 + 1, b0 + 2
                for nh, (W, wn) in enumerate(((Wg0, n0), (Wg1, n1))):
                    for c in range(8):
                        MM(PS[:, b0 + nh, :], aT[:, c, j * 128:(j + 1) * 128], W[:, c, :], c == 0, c == 7, [aTn[j]] + wn, [psb(b0 + nh)])
                tX, tn = (tmpA, 'tmpA') if j % 2 == 0 else (tmpB, 'tmpB')
                t3 = tX.rearrange("p (a b) -> p a b", b=512)
                TT('dve', t3, PS[:, bg0:bg0 + 2, :], bB.rearrange("p (a b) -> p a b", b=512), ALU.add, [psb(bg0), psb(bg1), bname], [tn])
                ACT(tX, tX, AF.Sigmoid, [tn], [tn])
                for nh in range(2):
                    for c2 in range(2):
                        MM(PS[:, bp0, :], pTsb[:, c2, j * 128:(j + 1) * 128], Wp[:, c2, nh * 512:(nh + 1) * 512], c2 == 0, c2 == 1,
                           [('pTsb', j)] + n2, [psb(bp0)])
                    TT('dve', tX[:, nh * 512:(nh + 1) * 512], tX[:, nh * 512:(nh + 1) * 512], PS[:, bp0, :], ALU.mult, [tn, psb(bp0)], [tn])
                TT('pool', h[:, j, :], h[:, j, :], tX, ALU.add, [('h', j), tn], [('h', j)])
            norm_pre(nt, gi_pre, slotwise=True, after_slot=per_slot)

        def kv_proj(nt, tile_qs):
            W, wn = wload(wkv_s, [128, 8, 512], 'wkv_s')

            def per_slot(j):
                qs = tile_qs[j]
                bk = 2 + j % 2
                for c in range(8):
                    MM(PS[:, bk, :], aT[:, c, j * 128:(j + 1) * 128], W[:, c, :], c == 0, c == 7, [aTn[j]] + wn, [psb(bk)])
                ACT(kvout, PS[:, bk, :], AF.Copy, [psb(bk)], ['kvout', psb(bk)])
                DMA('act', kvo_o[qs * 128:(qs + 1) * 128, :], kvout, ['kvout'], [('kvo', qs)])
                if qs == 33:
                    DMA('act', swks_o[112:128, :], kvout[0:16, 0:256], ['kvout'], [('swks', 1)])
                    DMA('act', swvs_o[112:128, :], kvout[0:16, 256:512], ['kvout'], [('swvs', 1)])
                cb = j + 1
                CP('dve', kvK, PS[:, bk, 0:256].rearrange("p (g d) -> p g d", d=64), [psb(bk)], ['kvK'])
                CP('dve', VAr[:, cb, :, 0:64], PS[:, bk, 256:512].rearrange("p (g d) -> p g d", d=64), [psb(bk)], ['VAr%d' % cb])
                tb = 4 + j % 2
                pv = ps_bf(tb, 1, 64, "p (g t) -> p g t", t=128)
                for g in range(4):
                    TR(pv[:, g, :], kvK[:, g, :], ['kvK'], [psb(tb)])
                CP('dve', KTr[0:64, cb, :, :], pv[:, 0:4, :], [psb(tb)], ['KTr%d' % cb])
            norm_pre(nt, 3, slotwise=True, after_slot=per_slot)

        for i in range(6):
            MS('pool', KTs[i], 0.0, ['kts%d' % i])
            MS('pool', VAs[i], 0.0, ['vas%d' % i])
        MS('pool', QTg, 0.0, ['QTg'])
        for ti, tile_qs in enumerate(TILES):
            nt = len(tile_qs)
            for j, qs in enumerate(tile_qs):
                src = xs if qs == 33 else xk[(31 + qs) * 128:(32 + qs) * 128, :]
                DMA('sp', h[:, j, :], src, [], [('h', j)])
            q_proj(0, nt, tile_qs, 0)
            groups = []
            if ti == 0:
                groups = [([0], 0, list(range(0, 31)), [31]), ([1], 33, list(range(LB_SC0, LB_S)), [LB_S])]
            else:
                groups = [(list(range(nt)), tile_qs[0], list(range(0, 31 + tile_qs[0])), [31 + q for q in tile_qs])]
            for poss, qref, pre_lbs, in_lbs in groups:
                lbref = lb_of(qref)
                for j in poss:
                    TT('dve', Qtm[:, j, :, 64:65], Fnk[:, lb_of(tile_qs[j]), :].unsqueeze(2), FrefB[:, lbref, :].unsqueeze(2), ALU.subtract,
                       ['Fnk', 'FrefB'], [('Qtm', j)])
            MS('pool', oT[64:128, :, :], 0.0, [('oT', i) for i in range(NH)])
            for poss, qref, pre_lbs, in_lbs in groups:
                fox_attn(poss, lb_of(qref), pre_lbs, in_lbs)
            out_proj(0, nt, 0)
            ffn(0, nt, 1, 1)
            ple(0, nt, tile_qs, 2)
            kv_proj(nt, tile_qs)
            q_proj(1, nt, tile_qs, 4)
            MS('pool', oT[64:128, :, :], 0.0, [('oT', i) for i in range(NH)])
            swa_attn(tile_qs)
            out_proj(1, nt, 2)
            ffn(1, nt, 5, 3)
            ple(1, nt, tile_qs, 6)
            for j, qs in enumerate(tile_qs):
                DMA('act', y_o[qs * 128:(qs + 1) * 128, :], h[:, j, :], [('h', j)], [('y', qs)])
            lastp = max(j for j, qs in enumerate(tile_qs) if qs != 33)
            CP('dve', KTr[0:64, 0, :, :], KTr[0:64, lastp + 1, :, :], ['KTr%d' % (lastp + 1)], ['KTr0'])
            CP('dve', VAr[:, 0, :, :], VAr[:, lastp + 1, :, :], ['VAr%d' % (lastp + 1)], ['VAr0'])

        outs = [('y', q) for q in range(NQS)] + [('fk', q) for q in range(NQS)] + [('fv', q) for q in range(NQS)] + \
               [('flf', q) for q in range(NQS)] + [('kvo', q) for q in range(NQS)] + [('swks', 0), ('swks', 1), ('swvs', 0), ('swvs', 1)]
        OP('sp', None, outs, [])
        S.emit(nc)
    return nc


_PROG = None


def _masks():
    k = np.arange(128)[:, None].astype(np.float64)
    q = np.arange(128)[None, :].astype(np.float64)
    m = np.zeros((65, 128, 128), np.float64)
    m[0] = np.where(k <= q, 0.0, NEG)
    for hd_ in range(16):
        sl = 2.0 ** (-8.0 * (hd_ + 1) / 16)
        m[1 + hd_] = np.where((q >= 64) & (k < 64), NEG, -sl * (q + 128 - k))
        m[17 + hd_] = np.where((q < 64) & (k >= 64), NEG, -sl * np.abs(q - k))
        m[33 + hd_] = -sl * (128 + q - k)
        m[49 + hd_] = np.where(k >= 16, NEG, -sl * np.abs(q - k))
    return np.ascontiguousarray(np.transpose(m, (1, 0, 2))).astype(np.float32)


def kernel(x_prompt, x_sample, cache_fox_k, cache_fox_v, cache_fox_logf, cache_swa_k, cache_swa_v,
           p_prompt, p_sample, norm_mix_pre, norm_mix_post, norm_ffn_pre, norm_ffn_post, fox_w_in, fox_b_f,
           fox_w_out, swa_w_q, swa_sinks, swa_w_out, kv_norm, swa_w_kv, ffn_w_gate, ffn_w_up, ffn_w_down,
           ple_norm, ple_w_gate, ple_b_gate, ple_w_proj):
    global _PROG
    f = lambda a: np.ascontiguousarray(np.asarray(a, dtype=np.float32))
    x_prompt = f(x_prompt); x_sample = f(x_sample); p_prompt = f(p_prompt); p_sample = f(p_sample)
    if _PROG is None:
        _PROG = build_program()
    nc = _PROG
    ident = np.eye(128, dtype=np.float32)
    umat = np.triu(np.ones((128, 128), np.float32))
    sel0 = np.zeros((128, 128), np.float32); sel0[0, :] = 1.0
    masks = _masks()
    pre = [f(norm_mix_pre)[0], f(norm_ffn_pre)[0], f(ple_norm)[0], f(kv_norm), f(norm_mix_pre)[1], f(norm_ffn_pre)[1], f(ple_norm)[1]]
    gpre = np.ascontiguousarray(np.stack([g.reshape(8, 128).T for g in pre], axis=1))
    gpost = np.ascontiguousarray(np.stack([f(norm_mix_post)[0], f(norm_ffn_post)[0], f(norm_mix_post)[1], f(norm_ffn_post)[1]]))
    common = dict(ident=ident, umat=umat, sel0=sel0, masks=masks, gpre=gpre, gpost=gpost, bgate=f(ple_b_gate),
                  bf=f(fox_b_f)[0], sinks=f(swa_sinks)[0], w_in=f(fox_w_in)[0], w_o0=f(fox_w_out)[0], w_q1=f(swa_w_q)[0],
                  w_o1=f(swa_w_out)[0], w_kv=f(swa_w_kv), w_g=f(ffn_w_gate), w_u=f(ffn_w_up), w_d=f(ffn_w_down),
                  w_pg=f(ple_w_gate), w_pp=f(ple_w_proj))
    in_maps = []
    for c in range(8):
        b, hf = c // 2, c % 2
        if hf == 1:
            xk = x_prompt[b]
            valid = np.ones(NBT, np.float32)
        else:
            xk = np.concatenate([np.zeros((32 * 128, D), np.float32), x_prompt[b, :4096]], axis=0)
            valid = np.ones(NBT, np.float32); valid[:32] = 0.0
        pown = np.zeros((2, NQS * 128, 256), np.float32)
        if hf == 1:
            pown[:, :33 * 128] = p_prompt[:, b, 31 * 128:]
        else:
            pown[:, 128:33 * 128] = p_prompt[:, b, :4096]
        pown[:, 33 * 128:33 * 128 + 16] = p_sample[:, c]
        xs = np.zeros((128, D), np.float32); xs[:16] = x_sample[c]
        swk = np.zeros(NQS, np.float32)
        if hf == 0:
            swk[1] = NEG
        m = dict(common)
        m.update(xk=np.ascontiguousarray(xk), xs=xs, pown=pown,
                 ck=f(cache_fox_k)[0, c].reshape(2048, D), cv=f(cache_fox_v)[0, c].reshape(2048, D),
                 clf=f(cache_fox_logf)[0, c], cswk=f(cache_swa_k)[c].reshape(128, 256), cswv=f(cache_swa_v)[c].reshape(128, 256),
                 negvalid=-valid, kill=(1.0 - valid) * 30000.0, swk=swk)
        in_maps.append(m)
    res = run_bass_kernel_spmd(nc, in_maps, core_ids=list(range(8)))
    R = res.results
    y_prompt = np.zeros((4, 8192, D), np.float32); y_sample = np.zeros((8, 16, D), np.float32)
    fkp = np.zeros((1, 4, 8192, NH, HD), np.float32); fvp = np.zeros_like(fkp); flp = np.zeros((1, 4, 8192, NH), np.float32)
    fks = np.zeros((1, 8, 16, NH, HD), np.float32); fvs = np.zeros_like(fks); fls = np.zeros((1, 8, 16, NH), np.float32)
    skp = np.zeros((4, 128, 4, HD), np.float32); svp = np.zeros_like(skp)
    sks = np.zeros((8, 128, 4, HD), np.float32); svs = np.zeros_like(sks)
    for c in range(8):
        b, hf = c // 2, c % 2
        r = R[c]
        sl = slice(hf * 4096, (hf + 1) * 4096)
        y_prompt[b, sl] = r['y'][128:33 * 128]
        y_sample[c] = r['y'][33 * 128:33 * 128 + 16]
        fkp[0, b, sl] = r['fk'][128:33 * 128].reshape(4096, NH, HD)
        fvp[0, b, sl] = r['fv'][128:33 * 128].reshape(4096, NH, HD)
        flp[0, b, sl] = r['flf'][128:33 * 128]
        fks[0, c] = r['fk'][33 * 128:33 * 128 + 16].reshape(16, NH, HD)
        fvs[0, c] = r['fv'][33 * 128:33 * 128 + 16].reshape(16, NH, HD)
        fls[0, c] = r['flf'][33 * 128:33 * 128 + 16]
        if hf == 1:
            skp[b] = r['kvo'][32 * 128:33 * 128, 0:256].reshape(128, 4, HD)
            svp[b] = r['kvo'][32 * 128:33 * 128, 256:512].reshape(128, 4, HD)
        sks[c] = r['swks'].reshape(128, 4, HD)
        svs[c] = r['swvs'].reshape(128, 4, HD)
    return (y_prompt, y_sample, fkp, fvp, flp, fks, fvs, fls, skp, svp, sks, svs)
```

```python
import contextlib
import numpy as np
import concourse.bass as bass
import concourse.mybir as mybir
from concourse.bass_utils import run_bass_kernel_spmd

F32 = mybir.dt.float32
BF16 = mybir.dt.bfloat16
AF = mybir.ActivationFunctionType
ALU = mybir.AluOpType

ENGS = ['pe', 'act', 'dve', 'pool', 'sp']
NPOOL = 48
QPOOL = {'sp': (0, 28), 'act': (28, 12), 'pool': (40, 8)}

D = 1024
NH = 16
HD = 64
DFF = 2816
NM = 22
NS = 33
NQS = 34
NLBP = 64
LB_SC0 = 64
LB_S = 80
NBT = 81
EPS = 1e-6
NEG = -30000.0
TILES = [[0, 33]] + [[4 * i - 3, 4 * i - 2, 4 * i - 1, 4 * i] for i in range(1, 9)]


def lb_of(qs):
    return 31 + qs if qs < 33 else LB_S


class Op:
    __slots__ = ('eng', 'fn', 'dma', 'pos', 'need_inc', 'dma_id', 'clock', 'waits', 'incval', 'dbg')


class Sched:
    def __init__(self):
        self.ops = {e: [] for e in ENGS}
        self.tiles = {}
        self.dma_ops = {e: [] for e in ENGS}
        self.clock = {e: {} for e in ENGS}
        self.iv = {}
        self.alias = {}

    def region(self, name, lo, hi, group=None):
        self.iv[name] = (lo, hi, group)
        self.alias[name] = []
        for n2, (l2, h2, g2) in self.iv.items():
            if n2 != name and lo < h2 and l2 < hi and not (group is not None and g2 == group):
                self.alias[name].append(n2)
                self.alias[n2].append(name)

    def _expand(self, names):
        out = []
        for n in names:
            out.append(n)
            al = self.alias.get(n)
            if al:
                out.extend(al)
        return out

    def _known(self, eng, d):
        c = self.clock[eng]
        if d.dma:
            return c.get(('d', d.dma_id[0]), -1) >= d.dma_id[1]
        return c.get(('c', d.eng), -1) >= d.pos

    def op(self, eng, fn, reads=(), writes=(), dma=False):
        o = Op()
        o.eng = eng; o.fn = fn; o.dma = dma; o.need_inc = False
        o.pos = len(self.ops[eng])
        o.dma_id = None
        o.dbg = (tuple(reads), tuple(writes))
        reads0, writes0 = list(reads), list(writes)
        reads = self._expand(reads)
        writes = self._expand(writes)
        deps = []
        for t in reads:
            st = self.tiles.get(t)
            if st is not None and st[0] is not None:
                deps.append((st[0], 'raw'))
        for t in writes:
            st = self.tiles.get(t)
            if st is not None:
                if st[0] is not None:
                    deps.append((st[0], 'waw'))
                for r in st[1]:
                    deps.append((r, 'war'))
        if dma:
            base, size = QPOOL[eng]
            n = len(self.dma_ops[eng])
            o.dma_id = (base + n % size, n // size)
            if n >= size:
                deps.append((self.dma_ops[eng][n - size], 'guard'))
            self.dma_ops[eng].append(o)
        waits = []
        for d, kind in deps:
            if d is o:
                continue
            if (not d.dma) and (not dma) and d.eng == eng and eng == 'pe':
                continue
            if self._known(eng, d):
                continue
            waits.append(d)
            d.need_inc = True
            c = dict(self.clock[eng])
            for k, v in d.clock.items():
                if c.get(k, -1) < v:
                    c[k] = v
            k = ('d', d.dma_id[0]) if d.dma else ('c', d.eng)
            v = d.dma_id[1] if d.dma else d.pos
            if c.get(k, -1) < v:
                c[k] = v
            self.clock[eng] = c
        o.waits = waits
        o.clock = self.clock[eng]
        for t in reads0:
            st = self.tiles.get(t)
            if st is None:
                self.tiles[t] = [None, [o]]
            else:
                st[1].append(o)
        for t in writes0:
            self.tiles[t] = [o, []]
        self.ops[eng].append(o)
        return o

    def emit(self, nc):
        with contextlib.ExitStack() as es:
            csem = {e: es.enter_context(nc.semaphore('s_' + e)) for e in ENGS}
            dsem = [es.enter_context(nc.semaphore('d%d' % i)) for i in range(NPOOL)]
            for e in ENGS:
                c = 0
                for o in self.ops[e]:
                    if o.dma:
                        o.incval = 16 * (o.dma_id[1] + 1)
                    elif o.need_inc:
                        c += 1
                        o.incval = c
            block = es.enter_context(nc.Block())

            def run(engname, eng):
                for o in self.ops[engname]:
                    wmap = {}
                    for d in o.waits:
                        key = ('d', d.dma_id[0]) if d.dma else ('c', d.eng)
                        if key not in wmap or wmap[key] < d.incval:
                            wmap[key] = d.incval
                    for key, val in wmap.items():
                        sem = dsem[key[1]] if key[0] == 'd' else csem[key[1]]
                        eng.wait_ge(sem, val)
                    if o.fn is None:
                        continue
                    ins = o.fn(eng)
                    if o.dma:
                        ins.then_inc(dsem[o.dma_id[0]], 16)
                    elif o.need_inc:
                        ins.then_inc(csem[engname], 1)

            @block.tensor
            def _(pe):
                run('pe', pe)

            @block.scalar
            def _(act):
                run('act', act)

            @block.vector
            def _(dve):
                run('dve', dve)

            @block.gpsimd
            def _(pool):
                run('pool', pool)

            @block.sync
            def _(sp):
                run('sp', sp)


def build_program():
    nc = bass.Bass("TRN2", target_bir_lowering=False)
    S = Sched()

    def din(name, shape, dt=F32):
        return nc.dram_tensor(name, list(shape), dt, kind="ExternalInput").ap()

    def dout(name, shape):
        return nc.dram_tensor(name, list(shape), F32, kind="ExternalOutput").ap()

    def dscr(name, shape, dt=BF16):
        return nc.dram_tensor(name, list(shape), dt, kind="Internal").ap()

    xk = din("xk", [NLBP * 128, D])
    xs = din("xs", [128, D])
    pown = din("pown", [2, NQS * 128, 256])
    ck = din("ck", [2048, D]); cv = din("cv", [2048, D]); clf = din("clf", [2048, NH])
    cswk = din("cswk", [128, 256]); cswv = din("cswv", [128, 256])
    negvalid_d = din("negvalid", [NBT]); kill_d = din("kill", [NBT]); swk_d = din("swk", [NQS])
    ident_d = din("ident", [128, 128]); umat_d = din("umat", [128, 128]); sel0_d = din("sel0", [128, 128])
    masks_d = din("masks", [128, 65, 128])
    gpre_d = din("gpre", [128, 7, 8]); gpost_d = din("gpost", [4, D]); bgate_d = din("bgate", [2, D])
    bf_d = din("bf", [NH]); sinks_d = din("sinks", [NH])
    w_in = din("w_in", [D, 3088]); w_o0 = din("w_o0", [D, D]); w_q1 = din("w_q1", [D, D]); w_o1 = din("w_o1", [D, D])
    w_kv = din("w_kv", [D, 512])
    w_g = din("w_g", [2, D, DFF]); w_u = din("w_u", [2, D, DFF]); w_d = din("w_d", [2, DFF, D])
    w_pg = din("w_pg", [2, D, D]); w_pp = din("w_pp", [2, 256, D])
    y_o = dout("y", [NQS * 128, D]); fk_o = dout("fk", [NQS * 128, D]); fv_o = dout("fv", [NQS * 128, D])
    flf_o = dout("flf", [NQS * 128, NH]); kvo_o = dout("kvo", [NQS * 128, 512])
    swks_o = dout("swks", [128, 256]); swvs_o = dout("swvs", [128, 256])
    kscr = dscr("kscr", [NBT, 65, NH, 128]); vscr = dscr("vscr", [NBT, 128, NH, 65])
    wkvf_s = dscr("wkvf_s", [128, 8, 2064])
    wq_s = [dscr("wq_s%d" % l, [2, 128, 8, 512]) for l in range(2)]
    wo_s = [dscr("wo_s%d" % l, [2, 2, 64, 8, 512]) for l in range(2)]
    wg_s = [dscr("wg_s%d" % l, [11, 128, 2, 8, 128]) for l in range(2)]
    wu_s = [dscr("wu_s%d" % l, [11, 128, 2, 8, 128]) for l in range(2)]
    wd_s = [dscr("wd_s%d" % l, [2, 2, 128, 11, 512]) for l in range(2)]
    wpg_s = [dscr("wpg_s%d" % l, [2, 128, 8, 512]) for l in range(2)]
    wpp_s = [dscr("wpp_s%d" % l, [128, 2, D]) for l in range(2)]
    wkv_s = dscr("wkv_s", [128, 8, 512])

    es = contextlib.ExitStack()
    with es:
        def sb(name, shape, dt):
            return es.enter_context(nc.sbuf_tensor("sb_" + name, list(shape), dt))

        PS = es.enter_context(nc.psum_tensor("ps", [128, 8, 512], F32))
        identf = sb("identf", [128, 128], F32); identb = sb("identb", [128, 128], BF16)
        umat = sb("umat", [128, 128], F32); onesf = sb("onesf", [128, 128], F32); sel0 = sb("sel0", [128, 128], F32)
        masks = sb("masks", [128, 65, 128], BF16)
        gpre = sb("gpre", [128, 7, 8], F32)
        bfB = sb("bfB", [128, NH], F32)
        negvalid = sb("negvalidB", [128, NBT], F32); killB = sb("killB", [128, NBT], F32); swkB = sb("swkB", [128, NQS], F32)
        esink = sb("esink", [65, NH], F32)
        Fnk = sb("Fnk", [128, NBT, NH], F32); Fs = sb("Fs", [128, NBT, NH], F32); FrefB = sb("FrefB", [128, NBT, NH], F32)
        h = sb("h", [128, 4, D], F32)
        sst = sb("sst", [128, 40], F32); lnv = sb("lnv", [128, 4], F32); rstd = sb("rstd", [128, 4], F32)
        biasb = sb("biasb", [128, 32], F32)
        zt = sb("zt", [128, 3, NH], F32)
        KTr = sb("KTr", [128, 6, 4, 128], BF16); VAr = sb("VAr", [128, 7, 4, 65], BF16)
        WPB = 12288
        NARB = 3 * WPB + 92 * 1024
        AR = sb("arena", [128, NARB // 2], BF16)

        class Bump:
            def __init__(self, off):
                self.off = off

            def take(self, name, nbytes, shape, dt, parts=128, pattern=None, names=None, **kw):
                lo = self.off
                nbytes = (nbytes + 31) // 32 * 32
                self.off += nbytes
                assert self.off <= NARB, (name, self.off)
                v = AR[0:parts, lo // 2:(lo + nbytes) // 2]
                if dt == F32:
                    v = v.bitcast(F32)
                n = 1
                for s_ in shape[1:]:
                    n *= s_
                v = v[:, 0:n]
                if pattern:
                    v = v.rearrange(pattern, **kw)
                if names is None:
                    S.region(name, lo, lo + nbytes)
                else:
                    for nm in names:
                        S.region(nm, lo, lo + nbytes, group=name)
                return v

        WSL = 4096
        NWS = 3 * WPB // WSL
        for i in range(NWS):
            S.region('ws%d' % i, i * WSL, (i + 1) * WSL)
        S.region('wkvf', 0, 3 * WPB)
        wkvf = AR[:, 0:8 * 2064].rearrange("p (c n) -> p c n", n=2064)
        wp_ctr = [0]

        def wview(slot, shape, parts=128):
            n = 1
            for s_ in shape[1:]:
                n *= s_
            v = AR[0:parts, slot * WSL // 2: slot * WSL // 2 + n]
            if len(shape) == 3:
                v = v.rearrange("p (a b) -> p a b", b=shape[2])
            elif len(shape) == 4:
                v = v.rearrange("p (a b c) -> p a b c", b=shape[2], c=shape[3])
            return v

        B0 = 3 * WPB
        bp = Bump(B0)
        xt = [bp.take('xt%d' % i, 4096, [128, D], F32) for i in range(4)]
        hnp2 = [bp.take('hnp%d' % i, 2048, [128, D], BF16) for i in range(3)]
        hnp = hnp2[0]
        aTb2 = [bp.take('aTb%d' % i, 2048, [128, 8, 128], BF16, pattern="p (c t) -> p c t", t=128) for i in range(3)]
        kout = bp.take('kout', 4096, [128, D], F32)
        vout = bp.take('vout', 4096, [128, D], F32)
        ktm = [bp.take('ktm%d' % i, 2080, [128, NH, 65], BF16, pattern="p (h d) -> p h d", d=65) for i in range(2)]
        va = [bp.take('va%d' % i, 2080, [128, NH, 65], BF16, pattern="p (h d) -> p h d", d=65) for i in range(2)]
        kTsb = bp.take('kTsb', 4096, [65, NH, 128], BF16, parts=65, pattern="p (h t) -> p h t", t=128)
        LF = bp.take('LF', NBT * NH * 4, [128, NBT, NH], F32, pattern="p (b h) -> p b h", h=NH, names=[('LF', lb) for lb in range(NBT)])
        OFF = bp.take('OFF', NBT * NH * 4, [128, NBT, NH], F32, pattern="p (b h) -> p b h", h=NH)
        bt = Bump(B0)
        aT = bt.take('aTall', 8192, [128, 8, 512], BF16, pattern="p (c t) -> p c t", t=512, names=['aT%d' % j for j in range(4)])
        hn = [bt.take('hn%d' % i, 2048, [128, D], BF16) for i in range(2)]
        Qtm = bt.take('Qtm', 4 * NH * 65 * 2, [128, 4, NH, 65], BF16, pattern="p (j h d) -> p j h d", h=NH, d=65, names=[('Qtm', j) for j in range(4)])
        tmpA = bt.take('tmpA', 4096, [128, D], F32)
        tmpB = bt.take('tmpB', 4096, [128, D], F32)
        gbuf = [bt.take('gbuf%d' % i, 4096, [128, D], F32) for i in range(2)]
        ptile = bt.take('ptile', 4096, [128, 4, 256], F32, pattern="p (j n) -> p j n", n=256, names=[('ptile', j) for j in range(4)])
        pbf = bt.take('pbf', 2048, [128, 4, 256], BF16, pattern="p (j n) -> p j n", n=256, names=[('pbf', j) for j in range(4)])
        pTsb = bt.take('pTsb', 2048, [128, 2, 512], BF16, pattern="p (c t) -> p c t", t=512, names=[('pTsb', j) for j in range(4)])
        kvout = bt.take('kvout', 2048, [128, 512], F32)
        kvK = bt.take('kvK', 512, [128, 4, 64], BF16, pattern="p (g d) -> p g d", d=64)
        O65 = [bt.take('o65_%d' % i, 2048, [65, 512], F32, parts=65) for i in range(4)]
        sg = [bt.take('sg%d' % i, 1024, [128, 512], BF16) for i in range(2)]
        KTs = [bt.take('kts%d' % i, 1024, [128, 4, 128], BF16, pattern="p (h t) -> p h t", t=128) for i in range(6)]
        VAs = [bt.take('vas%d' % i, 672, [128, 336], BF16) for i in range(6)]
        attn_off = bt.off
        QTg = bt.take('QTg', 4096, [128, 4, 512], BF16, pattern="p (h t) -> p h t", t=512)
        oT = bt.take('oT', 16384, [128, NH, 512], BF16, pattern="p (h t) -> p h t", t=512, names=[('oT', i) for i in range(NH)])
        PT = [bt.take('pt%d' % i, 1024, [128, 512], BF16) for i in range(4)]
        bh = Bump(attn_off)
        hidT = bh.take('hidT', NM * 512 * 2, [128, NM, 512], BF16, pattern="p (m t) -> p m t", t=512, names=[('hidT', m) for m in range(NM)])
        aTn = ['aT%d' % j for j in range(4)]

        def OP(eng, fn, r=(), w=(), dma=False):
            return S.op(eng, fn, reads=list(r), writes=list(w), dma=dma)

        def MM(out, lhsT, rhs, st, sp_, r, w):
            OP('pe', lambda e, o=out, l=lhsT, rr=rhs: e.matmul(o, lhsT=l, rhs=rr, start=st, stop=sp_), r, w)

        def TR(out, in_, r, w):
            OP('pe', lambda e, o=out, i=in_: e.transpose(out=o, in_=i, identity=identb[:]), list(r) + ['identb'], w)

        def DMA(eng, out, in_, r, w):
            OP(eng, lambda e, o=out, i=in_: e.dma_start(out=o, in_=i), r, w, dma=True)

        def ACT(out, in_, func, r, w, **kw):
            OP('act', lambda e, o=out, i=in_: e.activation(out=o, in_=i, func=func, **kw), r, w)

        def TT(eng, out, in0, in1, op, r, w):
            OP(eng, lambda e, o=out, a=in0, b=in1: e.tensor_tensor(out=o, in0=a, in1=b, op=op), r, w)

        def TS(eng, out, in0, s1, s2, op0, op1, r, w):
            if op1 is None:
                OP(eng, lambda e, o=out, a=in0: e.tensor_scalar(out=o, in0=a, scalar1=s1, scalar2=None, op0=op0), r, w)
            else:
                OP(eng, lambda e, o=out, a=in0: e.tensor_scalar(out=o, in0=a, scalar1=s1, scalar2=s2, op0=op0, op1=op1), r, w)

        def CP(eng, out, in_, r, w):
            OP(eng, lambda e, o=out, i=in_: e.tensor_copy(out=o, in_=i), r, w)

        def MS(eng, ap, val, w):
            OP(eng, lambda e, a=ap: e.memset(a, val), (), w)

        def psb(b):
            return 'ps%d' % b

        def ps_bf(b0, nb, parts, pattern, **kw):
            v = PS[0:parts, b0:b0 + nb, :].rearrange("p a b -> p (a b)").bitcast(BF16)
            return v.rearrange(pattern, **kw)

        allreg = list(S.iv.keys())
        MS('pool', AR[:, 0:NARB // 4], 0.0, allreg)
        MS('dve', AR[:, NARB // 4:NARB // 2], 0.0, allreg)
        DMA('sp', identf[:], ident_d, [], ['identf'])
        DMA('sp', umat[:], umat_d, [], ['umat'])
        DMA('sp', sel0[:], sel0_d, [], ['sel0'])
        DMA('sp', gpre[:], gpre_d, [], ['gpre'])
        DMA('sp', bfB[:], bf_d.partition_broadcast(128), [], ['bfB'])
        DMA('sp', negvalid[:], negvalid_d.partition_broadcast(128), [], ['negvalid'])
        DMA('sp', killB[:], kill_d.partition_broadcast(128), [], ['killB'])
        DMA('sp', swkB[:], swk_d.partition_broadcast(128), [], ['swkB'])
        DMA('sp', esink[64:65, :], sinks_d.rearrange("(o n) -> o n", o=1), [], ['esink'])
        DMA('pool', masks[:], masks_d, [], ['masks'])
        CP('dve', identb[:], identf[:], ['identf'], ['identb'])
        MS('pool', onesf[:], 1.0, ['onesf'])
        MS('pool', KTr[:], 0.0, ['KTr0', 'KTr1', 'KTr2', 'KTr3', 'KTr4', 'KTr5'])
        MS('pool', VAr[:], 1.0, ['VAr0', 'VAr1', 'VAr2', 'VAr3', 'VAr4', 'VAr5'])
        VArf = VAr[:].rearrange("p b g d -> p (b g d)")
        MS('pool', VAr[:, 0, :, 0:64], 0.0, ['VAr0'])
        for i in range(2):
            MS('pool', ktm[i][:, :, 64:65], 1.0, ['ktm%d' % i])
            MS('pool', va[i][:, :, 64:65], 1.0, ['va%d' % i])
        ACT(esink[64:65, :], esink[64:65, :], AF.Exp, ['esink'], ['esink'])

        def conv(dst, src, name):
            DMA('pool', dst, src, [], [name])

        for c in range(8):
            conv(wkvf_s[:, c, :], w_in[c * 128:(c + 1) * 128, 1024:3088], ('wkvf_s', c))
        for hf in range(2):
            conv(wq_s[0][hf], w_in[:, hf * 512:(hf + 1) * 512].rearrange("(c p) n -> p c n", p=128), ('wq_s', 0, hf))

        def conv_wo(l, src):
            for nh in range(2):
                for hh in range(2):
                    conv(wo_s[l][nh, hh], src[hh * 512:(hh + 1) * 512, nh * 512:(nh + 1) * 512].rearrange("(h d) n -> d h n", d=64), ('wo_s', l, nh, hh))

        def conv_ffn(l):
            for gi in range(11):
                for mm_ in range(2):
                    m = gi * 2 + mm_
                    conv(wg_s[l][gi, :, mm_], w_g[l, :, m * 128:(m + 1) * 128].rearrange("(c p) n -> p c n", p=128), ('wg_s', l, gi, mm_))
                    conv(wu_s[l][gi, :, mm_], w_u[l, :, m * 128:(m + 1) * 128].rearrange("(c p) n -> p c n", p=128), ('wu_s', l, gi, mm_))
            for nh in range(2):
                for mh in range(2):
                    conv(wd_s[l][nh, mh], w_d[l, mh * 1408:(mh + 1) * 1408, nh * 512:(nh + 1) * 512].rearrange("(m p) n -> p m n", p=128), ('wd_s', l, nh, mh))

        def conv_ple(l):
            for nh in range(2):
                conv(wpg_s[l][nh], w_pg[l, :, nh * 512:(nh + 1) * 512].rearrange("(c p) n -> p c n", p=128), ('wpg_s', l, nh))
            conv(wpp_s[l], w_pp[l].rearrange("(c p) n -> p c n", p=128), ('wpp_s', l))

        conv_wo(0, w_o0)
        conv_ffn(0)
        conv_ple(0)
        conv(wkv_s, w_kv.rearrange("(c p) n -> p c n", p=128), 'wkv_s')
        for hf in range(2):
            conv(wq_s[1][hf], w_q1[:, hf * 512:(hf + 1) * 512].rearrange("(c p) n -> p c n", p=128), ('wq_s', 1, hf))
        conv_wo(1, w_o1)
        conv_ffn(1)
        conv_ple(1)

        DMA('sp', wkvf, wkvf_s, [('wkvf_s', c) for c in range(8)], ['wkvf'])
        psT0 = ps_bf(0, 1, 128, "p (c t) -> p c t", t=128)
        pskT = ps_bf(6, 2, 65, "p (h t) -> p h t", t=128)

        def kv_to_scratch(lb, i, own_qs):
            DMA('act', vscr[lb], va[i], ['va%d' % i], [('vscr', lb)])
            for hh in range(NH):
                TR(pskT[:, hh, :], ktm[i][:, hh, :], ['ktm%d' % i], [psb(6), psb(7)])
            CP('dve', kTsb, pskT, [psb(6), psb(7)], ['kTsb'])
            DMA('act', kscr[lb], kTsb, ['kTsb'], [('kscr', lb)])

        xblocks = [(lb, xk[lb * 128:(lb + 1) * 128, :], (lb - 31) if lb >= 31 else None) for lb in range(NLBP)]
        xblocks.append((LB_S, xs, 33))
        def stageA1(bi):
            lb, src, own_qs = xblocks[bi]
            x4 = bi % 4
            hb = hnp2[bi % 3]
            hname = 'hnp%d' % (bi % 3)
            sn = 'sstp%d' % x4
            DMA('sp', xt[x4], src, [], ['xt%d' % x4])
            MS('pool', sst[:, 32 + x4:33 + x4], 0.0, [sn])
            ACT(hb, xt[x4], AF.Square, ['xt%d' % x4], [hname, sn], scale=1.0 / 32, accum_out=sst[:, 32 + x4:33 + x4])
            TS('dve', sst[:, 32 + x4:33 + x4], sst[:, 32 + x4:33 + x4], EPS, None, ALU.add, None, [sn], [sn])
            ACT(sst[:, 36 + x4:37 + x4], sst[:, 32 + x4:33 + x4], AF.Ln, [sn], [sn + 'l'])
            ACT(sst[:, 36 + x4:37 + x4], sst[:, 36 + x4:37 + x4], AF.Exp, [sn + 'l'], [sn + 'l'], scale=-0.5)
            ACT(hb, xt[x4], AF.Copy, ['xt%d' % x4, sn + 'l'], [hname], scale=sst[:, 36 + x4:37 + x4])

        def stageA2(bi):
            hb = hnp2[bi % 3]
            hname = 'hnp%d' % (bi % 3)
            x3 = bi % 3
            for c in range(8):
                TR(psT0[:, c, :], hb[:, c * 128:(c + 1) * 128], [hname], [psb(0)])
            TT('dve', aTb2[x3], psT0, gpre[:, 0, :].unsqueeze(2).to_broadcast([128, 8, 128]), ALU.mult, [psb(0), 'gpre'], ['aTb%d' % x3])

        def stageB(bi):
            lb, src, own_qs = xblocks[bi]
            i = bi % 2
            aTb = aTb2[bi % 3]
            an = 'aTb%d' % (bi % 3)
            for (bk, c0, c1) in ((1, 0, 512), (2, 512, 1024), (3, 1024, 1536), (4, 1536, 2048), (5, 2048, 2064)):
                for c in range(8):
                    MM(PS[:, bk, 0:c1 - c0], aTb[:, c, :], wkvf[:, c, c0:c1], c == 0, c == 7, [an, 'wkvf'], [psb(bk)])
            psk = PS[:, 1:3, :]
            psv = PS[:, 3:5, :]
            CP('dve', ktm[i][:, :, 0:64], psk.rearrange("p a (h d) -> p (a h) d", d=64), [psb(1), psb(2)], ['ktm%d' % i])
            CP('dve', va[i][:, :, 0:64], psv.rearrange("p a (h d) -> p (a h) d", d=64), [psb(3), psb(4)], ['va%d' % i])
            if own_qs is not None:
                ACT(kout.rearrange("p (a b) -> p a b", b=512), psk, AF.Copy, [psb(1), psb(2)], ['kout', psb(1), psb(2)])
                DMA('act', fk_o[own_qs * 128:(own_qs + 1) * 128, :], kout, ['kout'], [('fk', own_qs)])
                ACT(vout.rearrange("p (a b) -> p a b", b=512), psv, AF.Copy, [psb(3), psb(4)], ['vout', psb(3), psb(4)])
                DMA('act', fv_o[own_qs * 128:(own_qs + 1) * 128, :], vout, ['vout'], [('fv', own_qs)])
            TT('dve', zt[:, 0, :], PS[:, 5, 0:NH], bfB[:], ALU.add, [psb(5), 'bfB'], ['zt0'])
            ACT(zt[:, 1, :], zt[:, 0, :], AF.Exp, ['zt0'], ['zt1'], scale=-1.0)
            TS('dve', zt[:, 1, :], zt[:, 1, :], 1.0, None, ALU.add, None, ['zt1'], ['zt1'])
            ACT(zt[:, 2, :], zt[:, 1, :], AF.Ln, ['zt1'], ['zt2'])
            TS('dve', LF[:, lb, :], zt[:, 2, :], negvalid[:, lb:lb + 1], None, ALU.mult, None, ['zt2', 'negvalid'], [('LF', lb)])
            if own_qs is not None:
                DMA('act', flf_o[own_qs * 128:(own_qs + 1) * 128, :], LF[:, lb, :], [('LF', lb)], [('flf', own_qs)])
            kv_to_scratch(lb, i, own_qs)

        nblk = len(xblocks)
        stageA1(0); stageA1(1); stageA1(2)
        stageA2(0); stageA2(1)
        for bi in range(nblk):
            if bi + 3 < nblk:
                stageA1(bi + 3)
            if bi + 2 < nblk:
                stageA2(bi + 2)
            stageB(bi)
        for cb in range(16):
            lb = LB_SC0 + cb
            i = cb % 2
            DMA('sp', kout, ck[cb * 128:(cb + 1) * 128, :], [], ['kout'])
            DMA('sp', vout, cv[cb * 128:(cb + 1) * 128, :], [], ['vout'])
            DMA('sp', LF[:, lb, :], clf[cb * 128:(cb + 1) * 128, :], [], [('LF', lb)])
            CP('dve', ktm[i][:, :, 0:64], kout.rearrange("p (h d) -> p h d", d=64), ['kout'], ['ktm%d' % i])
            CP('dve', va[i][:, :, 0:64], vout.rearrange("p (h d) -> p h d", d=64), ['vout'], ['va%d' % i])
            kv_to_scratch(lb, i, None)
        DMA('sp', kout[:, 0:256], cswk, [], ['kout'])
        DMA('sp', vout[:, 0:256], cswv, [], ['vout'])
        DMA('sp', swks_o[0:112, :], cswk[16:128, :], [], [('swks', 0)])
        DMA('sp', swvs_o[0:112, :], cswv[16:128, :], [], [('swvs', 0)])
        CP('dve', hnp[:, 0:256], kout[:, 0:256], ['kout'], ['hnp0'])
        CP('dve', VAr[:, 5, :, 0:64], vout[:, 0:256].rearrange("p (g d) -> p g d", d=64), ['vout'], ['VAr5'])
        psk4 = ps_bf(0, 1, 64, "p (g t) -> p g t", t=128)
        for g in range(4):
            TR(psk4[:, g, :], hnp[:, g * 64:(g + 1) * 64], ['hnp0'], [psb(0)])
        CP('dve', KTr[0:64, 5, :, :], psk4[:, 0:4, :], [psb(0)], ['KTr5'])

        allLF = [('LF', lb) for lb in range(NBT)]
        LFf = LF.rearrange("p b h -> p (b h)")
        NCOL = NBT * NH
        chunks = [(0, 512), (512, 1024), (1024, NCOL)]
        for ci, (c0, c1) in enumerate(chunks):
            MM(PS[:, ci, 0:c1 - c0], umat[:], LFf[:, c0:c1], True, True, allLF + ['umat'], [psb(ci)])
            MM(PS[:, 3 + ci, 0:c1 - c0], onesf[:], LFf[:, c0:c1], True, True, allLF + ['onesf'], [psb(3 + ci)])
        MS('dve', OFF[:, 0, :], 0.0, ['OFF'])
        MS('dve', OFF[:, LB_SC0, :], 0.0, ['OFF'])
        for b in range(1, NBT):
            if b == LB_SC0:
                continue
            pb = b - 1
            bank, col = 3 + (pb * NH) // 512, (pb * NH) % 512
            TT('dve', OFF[:, b, :], OFF[:, pb, :], PS[:, bank, col:col + NH], ALU.add, ['OFF', psb(bank)], ['OFF'])
        OFFf = OFF.rearrange("p b h -> p (b h)")
        Fnkf = Fnk[:].rearrange("p b h -> p (b h)")
        for ci, (c0, c1) in enumerate(chunks):
            TT('dve', Fnkf[:, c0:c1], PS[:, ci, 0:c1 - c0], OFFf[:, c0:c1], ALU.add, [psb(ci), 'OFF'], ['Fnk'])
        TT('dve', Fs[:], Fnk[:], killB[:].unsqueeze(2).to_broadcast([128, NBT, NH]), ALU.add, ['Fnk', 'killB'], ['Fs'])
        for ci, (c0, c1) in enumerate(chunks):
            MM(PS[:, 6, 0:c1 - c0], sel0[:], Fnkf[:, c0:c1], True, True, ['Fnk', 'sel0'], [psb(6)])
            CP('dve', FrefB[:].rearrange("p b h -> p (b h)")[:, c0:c1], PS[:, 6, 0:c1 - c0], [psb(6)], ['FrefB'])

        def wload(src, shape, srcname, parts=128):
            n = 2
            for s_ in shape[1:]:
                n *= s_
            ns = (n + WSL - 1) // WSL
            if wp_ctr[0] + ns > NWS:
                wp_ctr[0] = 0
            s0 = wp_ctr[0]
            wp_ctr[0] += ns
            v = wview(s0, shape, parts)
            names = ['ws%d' % i for i in range(s0, s0 + ns)]
            DMA('sp', v, src, srcname if isinstance(srcname, list) else [srcname], names)
            return v, names

        gb_ctr = [0]

        def gload(src_row):
            i = gb_ctr[0] % 2
            gb_ctr[0] += 1
            DMA('sp', gbuf[i], src_row.partition_broadcast(128), [], ['gbuf%d' % i])
            return gbuf[i], 'gbuf%d' % i

        def norm_pre(nt, gi):
            MS('pool', sst[:, 0:nt], 0.0, ['sst'])
            for j in range(nt):
                ACT(hn[j % 2], h[:, j, :], AF.Square, [('h', j)], ['hn%d' % (j % 2), 'sst'], scale=1.0 / 32, accum_out=sst[:, j:j + 1])
            TS('dve', sst[:, 0:nt], sst[:, 0:nt], EPS, None, ALU.add, None, ['sst'], ['sst'])
            ACT(lnv[:, 0:nt], sst[:, 0:nt], AF.Ln, ['sst'], ['lnv'])
            ACT(rstd[:, 0:nt], lnv[:, 0:nt], AF.Exp, ['lnv'], ['rstd'], scale=-0.5)
            for j in range(nt):
                if j % 2 == 0:
                    ACT(hn[j % 2], h[:, j, :], AF.Copy, [('h', j), 'rstd'], ['hn%d' % (j % 2)], scale=rstd[:, j:j + 1])
                else:
                    TS('dve', hn[j % 2], h[:, j, :], rstd[:, j:j + 1], None, ALU.mult, None, [('h', j), 'rstd'], ['hn%d' % (j % 2)])
                bk = j % 2
                pv = ps_bf(bk, 1, 128, "p (c t) -> p c t", t=128)
                for c in range(8):
                    TR(pv[:, c, :], hn[j % 2][:, c * 128:(c + 1) * 128], ['hn%d' % (j % 2)], [psb(bk)])
                TT('dve', aT[:, :, j * 128:(j + 1) * 128], pv, gpre[:, gi, :].unsqueeze(2).to_broadcast([128, 8, 128]), ALU.mult,
                   [psb(bk), 'gpre'], [aTn[j]])

        def post_norm_all(nt, bankfn, gB, gname):
            MS('pool', sst[:, 8:16], 0.0, [('pn', j) for j in range(4)])
            tmps = (tmpA, tmpB)
            tnames = ('tmpA', 'tmpB')
            for j in range(nt):
                b0, b1 = bankfn(j)
                jn = hn[j % 2]
                ACT(jn[:, 0:512], PS[:, b0, :], AF.Square, [psb(b0)], ['hn%d' % (j % 2), ('pn', j), psb(b0)], scale=1.0 / 32, accum_out=sst[:, 8 + 2 * j:9 + 2 * j])
                ACT(jn[:, 512:1024], PS[:, b1, :], AF.Square, [psb(b1)], ['hn%d' % (j % 2), ('pn', j), psb(b1)], scale=1.0 / 32, accum_out=sst[:, 9 + 2 * j:10 + 2 * j])
                t = tmps[j % 2]
                tn = tnames[j % 2]
                for hf, bk in enumerate((b0, b1)):
                    TT('dve', t[:, hf * 512:(hf + 1) * 512], PS[:, bk, :], gB[:, hf * 512:(hf + 1) * 512], ALU.mult, [psb(bk), gname], [tn])
                TS('dve', sst[:, 16 + j:17 + j], sst[:, 8 + 2 * j:9 + 2 * j], sst[:, 9 + 2 * j:10 + 2 * j], EPS, ALU.add, ALU.add, [('pn', j)], [('pn2', j)])
                ACT(sst[:, 20 + j:21 + j], sst[:, 16 + j:17 + j], AF.Ln, [('pn2', j)], [('pn3', j)])
                ACT(sst[:, 24 + j:25 + j], sst[:, 20 + j:21 + j], AF.Exp, [('pn3', j)], [('pn4', j)], scale=-0.5)
                OP('dve', lambda e, j=j, t=t: e.scalar_tensor_tensor(out=h[:, j, :], in0=t, scalar=sst[:, 24 + j:25 + j], in1=h[:, j, :],
                                                                      op0=ALU.mult, op1=ALU.add),
                   [tn, ('pn4', j), ('h', j)], [('h', j)])

        kv_ctr = [0]; bias_ctr = [0]; pt_ctr = [0]; sbk_ctr = [0]; rl_ctr = [0]

        pending = []

        def normalize_heads(hg, gc0, gw, sink=False):
            for hh in range(4):
                ob = 4 + hh
                CP('dve', O65[hh][:, 0:gw], PS[0:65, ob, gc0:gc0 + gw], [psb(ob)], ['o65_%d' % hh])

            def part2(banks=None):
                for hh in range(4):
                    hd_ = hg * 4 + hh
                    on = 'o65_%d' % hh
                    if sink:
                        TS('dve', O65[hh][64:65, 0:gw], O65[hh][64:65, 0:gw], esink[64:65, hd_:hd_ + 1], None, ALU.add, None, [on, 'esink'], [on])
                    OP('dve', lambda e, hh=hh: e.reciprocal(out=O65[hh][64:65, 0:gw], in_=O65[hh][64:65, 0:gw]), [on], [on])
                for hh in range(4):
                    hd_ = hg * 4 + hh
                    on = 'o65_%d' % hh
                    if banks is None:
                        sbk = SB_BANKS[sbk_ctr[0] % 3]
                        sbk_ctr[0] += 1
                    else:
                        sbk = banks[hh % len(banks)]
                    MM(PS[0:64, sbk, 0:gw], onesf[64:65, 0:64], O65[hh][64:65, 0:gw], True, True, ['onesf', on], [psb(sbk)])
                    TT('dve', oT[0:64, hd_, gc0:gc0 + gw], O65[hh][0:64, 0:gw], PS[0:64, sbk, 0:gw], ALU.mult, [on, psb(sbk)], [('oT', hd_)])
            pending.append(part2)

        def flush_pending(banks=None):
            while pending:
                pending.pop(0)(banks)

        def q_transposes(hg, poss, rows):
            c0, c1 = poss[0] * 128, (poss[-1] + 1) * 128
            for half in range(2):
                psq = ps_bf(0, 1, 65, "p (h t) -> p h t", t=512)
                for j in poss:
                    for h2 in range(2):
                        hh = half * 2 + h2
                        TR(psq[0:rows, h2, j * 128:(j + 1) * 128], Qtm[:, j, hg * 4 + hh, 0:rows], [('Qtm', j)], [psb(0)])
                CP('dve', QTg[0:rows, half * 2:half * 2 + 2, c0:c1], psq[0:rows, :, c0:c1], [psb(0)], ['QTg'])

        SB_BANKS = [1, 2, 3]
        LAG = 2

        def fox_attn(poss, lbref, pre_lbs, in_lbs):
            gc0 = poss[0] * 128
            gw = len(poss) * 128
            steps = [(lb, 0, False) for lb in pre_lbs] + [(lb, jj * 128, True) for jj, lb in enumerate(in_lbs)]
            for hg in range(4):
                q_transposes(hg, poss, 65)
                items = [(si, hh) for si in range(len(steps)) for hh in range(4)]
                st = {}
                defer_at = min(32, len(items) // 2)
                for idx in range(len(items) + LAG):
                    if idx == defer_at:
                        flush_pending()
                    if idx < len(items):
                        si, hh = items[idx]
                        lb, q0, diag = steps[si]
                        if hh == 0:
                            kb = kv_ctr[0] % 6
                            kv_ctr[0] += 1
                            DMA('sp', KTs[kb][0:65], kscr[lb, :, hg * 4:(hg + 1) * 4, :], [('kscr', lb)], ['kts%d' % kb])
                            DMA('sp', VAs[kb][:, 0:260].rearrange("p (h d) -> p h d", d=65), vscr[lb, :, hg * 4:(hg + 1) * 4, :], [('vscr', lb)], ['vas%d' % kb])
                            bb = bias_ctr[0] % 8
                            bias_ctr[0] += 1
                            Ft, Fn = (Fnk, 'Fnk') if diag else (Fs, 'Fs')
                            TT('dve', biasb[:, bb * 4:(bb + 1) * 4], FrefB[:, lbref, hg * 4:(hg + 1) * 4], Ft[:, lb, hg * 4:(hg + 1) * 4], ALU.subtract,
                               ['FrefB', Fn], ['bias%d' % bb])
                            st[si] = (kb, bb)
                        kb, bb = st[si]
                        ncols = gw - q0
                        sbk = SB_BANKS[sbk_ctr[0] % 3]
                        sbk_ctr[0] += 1
                        MM(PS[:, sbk, 0:ncols], KTs[kb][:, hh, :], QTg[:, hh, gc0 + q0:gc0 + gw], True, not diag, ['kts%d' % kb, 'QTg'], [psb(sbk)])
                        if diag:
                            MM(PS[:, sbk, 0:128], identb[:], masks[:, 0, :], False, True, ['identb', 'masks'], [psb(sbk)])
                        pb = pt_ctr[0] % 4
                        pt_ctr[0] += 1
                        ACT(PT[pb][:, 0:ncols], PS[:, sbk, 0:ncols], AF.Exp, [psb(sbk), 'bias%d' % bb], ['pt%d' % pb],
                            bias=biasb[:, bb * 4 + hh:bb * 4 + hh + 1], scale=1.0)
                        st[(si, hh)] = pb
                    if idx >= LAG:
                        si, hh = items[idx - LAG]
                        lb, q0, diag = steps[si]
                        kb, bb = st[si]
                        pb = st[(si, hh)]
                        ncols = gw - q0
                        MM(PS[:, 4 + hh, gc0 + q0:gc0 + gw], VAs[kb][:, hh * 65:hh * 65 + 128], PT[pb][:, 0:ncols], si == 0, si == len(steps) - 1,
                           ['vas%d' % kb, 'pt%d' % pb], [psb(4 + hh)])
                normalize_heads(hg, gc0, gw)

        def swa_attn(tile_qs):
            nt = len(tile_qs)
            for hg in range(4):
                q_transposes(hg, list(range(nt)), 64)
                st = {}
                for idx in range(nt + 1):
                    if idx == 1:
                        flush_pending()
                    if idx < nt:
                        j = idx
                        qs = tile_qs[j]
                        if qs == 33:
                            pbuf, mprev, mcur = 5, 33 + hg * 4, 49 + hg * 4
                        else:
                            pbuf, mprev, mcur = (0 if j == 0 else j), 1 + hg * 4, 17 + hg * 4
                        cbuf = j + 1
                        qv = QTg[:, :, j * 128:(j + 1) * 128]
                        sb1 = SB_BANKS[sbk_ctr[0] % 3]
                        sb2 = SB_BANKS[(sbk_ctr[0] + 1) % 3]
                        sbk_ctr[0] += 2
                        MM(PS[:, sb1, :], KTr[:, pbuf, hg, :], qv, True, False, ['KTr%d' % pbuf, 'QTg'], [psb(sb1)])
                        MM(PS[:, sb1, :], identb[:], masks[:, mprev:mprev + 4, :], False, True, ['identb', 'masks'], [psb(sb1)])
                        MM(PS[:, sb2, :], KTr[:, cbuf, hg, :], qv, True, False, ['KTr%d' % cbuf, 'QTg'], [psb(sb2)])
                        MM(PS[:, sb2, :], identb[:], masks[:, mcur:mcur + 4, :], False, True, ['identb', 'masks'], [psb(sb2)])
                        p1 = pt_ctr[0] % 4
                        p2 = (pt_ctr[0] + 1) % 4
                        pt_ctr[0] += 2
                        ACT(PT[p1], PS[:, sb1, :], AF.Exp, [psb(sb1), 'swkB'], ['pt%d' % p1], bias=swkB[:, qs:qs + 1], scale=1.0)
                        ACT(PT[p2], PS[:, sb2, :], AF.Exp, [psb(sb2)], ['pt%d' % p2])
                        st[j] = (p1, p2, pbuf, cbuf)
                    if idx >= 1:
                        j = idx - 1
                        p1, p2, pbuf, cbuf = st[j]
                        MM(PS[:, 4 + j, :], VArf[:, (pbuf * 4 + hg) * 65:(pbuf * 4 + hg) * 65 + 128], PT[p1], True, False, ['VAr%d' % pbuf, 'pt%d' % p1], [psb(4 + j)])
                        MM(PS[:, 4 + j, :], VArf[:, (cbuf * 4 + hg) * 65:(cbuf * 4 + hg) * 65 + 128], PT[p2], False, True, ['VAr%d' % cbuf, 'pt%d' % p2], [psb(4 + j)])
                for j in range(nt):
                    CP('dve', O65[j], PS[0:65, 4 + j, :], [psb(4 + j)], ['o65_%d' % j])

                def part2(banks=None, hg=hg):
                    for j in range(nt):
                        on = 'o65_%d' % j
                        r3 = O65[j][64:65, :].rearrange("p (h q) -> p h q", q=128)
                        TT('dve', r3, r3, esink[64:65, hg * 4:(hg + 1) * 4].unsqueeze(2).to_broadcast([1, 4, 128]), ALU.add, [on, 'esink'], [on])
                        ACT(O65[j][64:65, :], O65[j][64:65, :], AF.Ln, [on], [on])
                        ACT(O65[j][64:65, :], O65[j][64:65, :], AF.Exp, [on], [on], scale=-1.0)
                    for j in range(nt):
                        on = 'o65_%d' % j
                        if banks is None:
                            sbk = SB_BANKS[sbk_ctr[0] % 3]
                            sbk_ctr[0] += 1
                        else:
                            sbk = banks[j % len(banks)]
                        MM(PS[0:64, sbk, :], onesf[64:65, 0:64], O65[j][64:65, :], True, True, ['onesf', on], [psb(sbk)])
                        TT('dve', oT[0:64, hg * 4:(hg + 1) * 4, j * 128:(j + 1) * 128], O65[j][0:64, :].rearrange("p (h q) -> p h q", q=128),
                           PS[0:64, sbk, :].rearrange("p (h q) -> p h q", q=128), ALU.mult, [on, psb(sbk)], [('oT', hg * 4 + i) for i in range(4)])
                pending.append(part2)

        def q_proj(l, nt, tile_qs):
            rot = 0
            for hf in range(2):
                W, wn = wload(wq_s[l][hf], [128, 8, 512], ('wq_s', l, hf))
                for j in range(nt):
                    bk = 2 + rot % 4
                    rot += 1
                    for c in range(8):
                        MM(PS[:, bk, :], aT[:, c, j * 128:(j + 1) * 128], W[:, c, :], c == 0, c == 7, [aTn[j]] + wn, [psb(bk)])
                    ACT(Qtm[:, j, hf * 8:(hf + 1) * 8, 0:64], PS[:, bk, :].rearrange("p (h d) -> p h d", d=64), AF.Copy, [psb(bk)], [('Qtm', j)], scale=0.125)

        def out_proj(l, nt, gidx):
            for nh in range(2):
                for hh2 in range(2):
                    if nh == 0 and hh2 == 1:
                        flush_pending(banks=[4, 5, 6, 7])
                    W, wn = wload(wo_s[l][nh, hh2], [64, 8, 512], ('wo_s', l, nh, hh2), parts=64)
                    W = wview(int(wn[0][2:]), [128, 8, 512])
                    for j in range(nt):
                        bk = nh * 4 + j
                        for i in range(8):
                            hd_ = hh2 * 8 + i
                            MM(PS[:, bk, :], oT[:, hd_, j * 128:(j + 1) * 128], W[:, i, :], hh2 == 0 and i == 0, hh2 == 1 and i == 7,
                               [('oT', hd_)] + wn, [psb(bk)])
            gB, gname = gload(gpost_d[gidx])
            post_norm_all(nt, lambda j: (j, 4 + j), gB, gname)

        def ffn(l, nt, gi_pre, gidx_post):
            ntk = nt * 128
            norm_pre(nt, gi_pre)
            for gi in range(11):
                Wg, wgn = wload(wg_s[l][gi], [128, 2, 8, 128], [('wg_s', l, gi, 0), ('wg_s', l, gi, 1)])
                Wu, wun = wload(wu_s[l][gi], [128, 2, 8, 128], [('wu_s', l, gi, 0), ('wu_s', l, gi, 1)])
                for mm_ in range(2):
                    m = gi * 2 + mm_
                    bg, bu = 2 * (m % 2), 2 * (m % 2) + 1
                    for c in range(8):
                        MM(PS[:, bg, 0:ntk], Wg[:, mm_, c, :], aT[:, c, 0:ntk], c == 0, c == 7, wgn + aTn[:nt], [psb(bg)])
                    for c in range(8):
                        MM(PS[:, bu, 0:ntk], Wu[:, mm_, c, :], aT[:, c, 0:ntk], c == 0, c == 7, wun + aTn[:nt], [psb(bu)])
                    ACT(sg[m % 2][:, 0:ntk], PS[:, bg, 0:ntk], AF.Silu, [psb(bg)], ['sg%d' % (m % 2)])
                    TT('dve', hidT[:, m, 0:ntk], sg[m % 2][:, 0:ntk], PS[:, bu, 0:ntk], ALU.mult, ['sg%d' % (m % 2), psb(bu)], [('hidT', m)])
            for nh in range(2):
                for mh in range(2):
                    W, wn = wload(wd_s[l][nh, mh], [128, 11, 512], ('wd_s', l, nh, mh))
                    for j in range(nt):
                        bk = (4 + j) if nh == 0 else j
                        for i in range(11):
                            m = mh * 11 + i
                            MM(PS[:, bk, :], hidT[:, m, j * 128:(j + 1) * 128], W[:, i, :], mh == 0 and i == 0, mh == 1 and i == 10,
                               [('hidT', m)] + wn, [psb(bk)])
            gB, gname = gload(gpost_d[gidx_post])
            post_norm_all(nt, lambda j: (4 + j, j), gB, gname)

        def ple(l, nt, tile_qs, gi_pre):
            norm_pre(nt, gi_pre)
            for j, qs in enumerate(tile_qs):
                DMA('sp', ptile[:, j, :], pown[l, qs * 128:(qs + 1) * 128, :], [], [('ptile', j)])
                CP('dve', pbf[:, j, :], ptile[:, j, :], [('ptile', j)], [('pbf', j)])
                pv = ps_bf(7, 1, 128, "p (c t) -> p c t", t=128)
                for c2 in range(2):
                    TR(pv[:, c2, :], pbf[:, j, c2 * 128:(c2 + 1) * 128], [('pbf', j)], [psb(7)])
                CP('dve', pTsb[:, :, j * 128:(j + 1) * 128], pv[:, 0:2, :], [psb(7)], [('pTsb', j)])
            Wg0, n0 = wload(wpg_s[l][0], [128, 8, 512], ('wpg_s', l, 0))
            Wg1, n1 = wload(wpg_s[l][1], [128, 8, 512], ('wpg_s', l, 1))
            Wp, n2 = wload(wpp_s[l], [128, 2, D], ('wpp_s', l))
            bB, bname = gload(bgate_d[l])
            for j in range(nt):
                b0 = (j % 2) * 4
                for nh, (W, wn) in enumerate(((Wg0, n0), (Wg1, n1))):
                    for c in range(8):
                        MM(PS[:, b0 + nh, :], aT[:, c, j * 128:(j + 1) * 128], W[:, c, :], c == 0, c == 7, [aTn[j]] + wn, [psb(b0 + nh)])
                for nh in range(2):
                    for c2 in range(2):
                        MM(PS[:, b0 + 2 + nh, :], pTsb[:, c2, j * 128:(j + 1) * 128], Wp[:, c2, nh * 512:(nh + 1) * 512], c2 == 0, c2 == 1,
                           [('pTsb', j)] + n2, [psb(b0 + 2 + nh)])
                tX, tn = (tmpA, 'tmpA') if j % 2 == 0 else (tmpB, 'tmpB')
                t3 = tX.rearrange("p (a b) -> p a b", b=512)
                TT('dve', t3, PS[:, b0:b0 + 2, :], bB.rearrange("p (a b) -> p a b", b=512), ALU.add, [psb(b0), psb(b0 + 1), bname], [tn])
                ACT(tX, tX, AF.Sigmoid, [tn], [tn])
                TT('dve', t3, t3, PS[:, b0 + 2:b0 + 4, :], ALU.mult, [tn, psb(b0 + 2), psb(b0 + 3)], [tn])
                TT('pool', h[:, j, :], h[:, j, :], tX, ALU.add, [('h', j), tn], [('h', j)])

        def kv_proj(nt, tile_qs):
            norm_pre(nt, 3)
            W, wn = wload(wkv_s, [128, 8, 512], 'wkv_s')
            for j, qs in enumerate(tile_qs):
                bk = j % 4
                for c in range(8):
                    MM(PS[:, bk, :], aT[:, c, j * 128:(j + 1) * 128], W[:, c, :], c == 0, c == 7, [aTn[j]] + wn, [psb(bk)])
                ACT(kvout, PS[:, bk, :], AF.Copy, [psb(bk)], ['kvout', psb(bk)])
                DMA('act', kvo_o[qs * 128:(qs + 1) * 128, :], kvout, ['kvout'], [('kvo', qs)])
                if qs == 33:
                    DMA('act', swks_o[112:128, :], kvout[0:16, 0:256], ['kvout'], [('swks', 1)])
                    DMA('act', swvs_o[112:128, :], kvout[0:16, 256:512], ['kvout'], [('swvs', 1)])
                cb = j + 1
                CP('dve', kvK, PS[:, bk, 0:256].rearrange("p (g d) -> p g d", d=64), [psb(bk)], ['kvK'])
                CP('dve', VAr[:, cb, :, 0:64], PS[:, bk, 256:512].rearrange("p (g d) -> p g d", d=64), [psb(bk)], ['VAr%d' % cb])
                tb = 4 + j % 2
                pv = ps_bf(tb, 1, 64, "p (g t) -> p g t", t=128)
                for g in range(4):
                    TR(pv[:, g, :], kvK[:, g, :], ['kvK'], [psb(tb)])
                CP('dve', KTr[0:64, cb, :, :], pv[:, 0:4, :], [psb(tb)], ['KTr%d' % cb])

        for i in range(6):
            MS('pool', KTs[i], 0.0, ['kts%d' % i])
            MS('pool', VAs[i], 0.0, ['vas%d' % i])
        MS('pool', QTg, 0.0, ['QTg'])
        for ti, tile_qs in enumerate(TILES):
            nt = len(tile_qs)
            for j, qs in enumerate(tile_qs):
                src = xs if qs == 33 else xk[(31 + qs) * 128:(32 + qs) * 128, :]
                DMA('sp', h[:, j, :], src, [], [('h', j)])
            norm_pre(nt, 0)
            q_proj(0, nt, tile_qs)
            groups = []
            if ti == 0:
                groups = [([0], 0, list(range(0, 31)), [31]), ([1], 33, list(range(LB_SC0, LB_S)), [LB_S])]
            else:
                groups = [(list(range(nt)), tile_qs[0], list(range(0, 31 + tile_qs[0])), [31 + q for q in tile_qs])]
            for poss, qref, pre_lbs, in_lbs in groups:
                lbref = lb_of(qref)
                for j in poss:
                    TT('dve', Qtm[:, j, :, 64:65], Fnk[:, lb_of(tile_qs[j]), :].unsqueeze(2), FrefB[:, lbref, :].unsqueeze(2), ALU.subtract,
                       ['Fnk', 'FrefB'], [('Qtm', j)])
            MS('pool', oT[64:128, :, :], 0.0, [('oT', i) for i in range(NH)])
            for poss, qref, pre_lbs, in_lbs in groups:
                fox_attn(poss, lb_of(qref), pre_lbs, in_lbs)
            out_proj(0, nt, 0)
            ffn(0, nt, 1, 1)
            ple(0, nt, tile_qs, 2)
            kv_proj(nt, tile_qs)
            norm_pre(nt, 4)
            q_proj(1, nt, tile_qs)
            MS('pool', oT[64:128, :, :], 0.0, [('oT', i) for i in range(NH)])
            swa_attn(tile_qs)
            out_proj(1, nt, 2)
            ffn(1, nt, 5, 3)
            ple(1, nt, tile_qs, 6)
            for j, qs in enumerate(tile_qs):
                DMA('act', y_o[qs * 128:(qs + 1) * 128, :], h[:, j, :], [('h', j)], [('y', qs)])
            lastp = max(j for j, qs in enumerate(tile_qs) if qs != 33)
            CP('dve', KTr[0:64, 0, :, :], KTr[0:64, lastp + 1, :, :], ['KTr%d' % (lastp + 1)], ['KTr0'])
            CP('dve', VAr[:, 0, :, :], VAr[:, lastp + 1, :, :], ['VAr%d' % (lastp + 1)], ['VAr0'])

        outs = [('y', q) for q in range(NQS)] + [('fk', q) for q in range(NQS)] + [('fv', q) for q in range(NQS)] + \
               [('flf', q) for q in range(NQS)] + [('kvo', q) for q in range(NQS)] + [('swks', 0), ('swks', 1), ('swvs', 0), ('swvs', 1)]
        OP('sp', None, outs, [])
        S.emit(nc)
    return nc


_PROG = None


def _masks():
    k = np.arange(128)[:, None].astype(np.float64)
    q = np.arange(128)[None, :].astype(np.float64)
    m = np.zeros((65, 128, 128), np.float64)
    m[0] = np.where(k <= q, 0.0, NEG)
    for hd_ in range(16):
        sl = 2.0 ** (-8.0 * (hd_ + 1) / 16)
        m[1 + hd_] = np.where((q >= 64) & (k < 64), NEG, -sl * (q + 128 - k))
        m[17 + hd_] = np.where((q < 64) & (k >= 64), NEG, -sl * np.abs(q - k))
        m[33 + hd_] = -sl * (128 + q - k)
        m[49 + hd_] = np.where(k >= 16, NEG, -sl * np.abs(q - k))
    return np.ascontiguousarray(np.transpose(m, (1, 0, 2))).astype(np.float32)


def kernel(x_prompt, x_sample, cache_fox_k, cache_fox_v, cache_fox_logf, cache_swa_k, cache_swa_v,
           p_prompt, p_sample, norm_mix_pre, norm_mix_post, norm_ffn_pre, norm_ffn_post, fox_w_in, fox_b_f,
           fox_w_out, swa_w_q, swa_sinks, swa_w_out, kv_norm, swa_w_kv, ffn_w_gate, ffn_w_up, ffn_w_down,
           ple_norm, ple_w_gate, ple_b_gate, ple_w_proj):
    global _PROG
    f = lambda a: np.ascontiguousarray(np.asarray(a, dtype=np.float32))
    x_prompt = f(x_prompt); x_sample = f(x_sample); p_prompt = f(p_prompt); p_sample = f(p_sample)
    if _PROG is None:
        _PROG = build_program()
    nc = _PROG
    ident = np.eye(128, dtype=np.float32)
    umat = np.triu(np.ones((128, 128), np.float32))
    sel0 = np.zeros((128, 128), np.float32); sel0[0, :] = 1.0
    masks = _masks()
    pre = [f(norm_mix_pre)[0], f(norm_ffn_pre)[0], f(ple_norm)[0], f(kv_norm), f(norm_mix_pre)[1], f(norm_ffn_pre)[1], f(ple_norm)[1]]
    gpre = np.ascontiguousarray(np.stack([g.reshape(8, 128).T for g in pre], axis=1))
    gpost = np.ascontiguousarray(np.stack([f(norm_mix_post)[0], f(norm_ffn_post)[0], f(norm_mix_post)[1], f(norm_ffn_post)[1]]))
    common = dict(ident=ident, umat=umat, sel0=sel0, masks=masks, gpre=gpre, gpost=gpost, bgate=f(ple_b_gate),
                  bf=f(fox_b_f)[0], sinks=f(swa_sinks)[0], w_in=f(fox_w_in)[0], w_o0=f(fox_w_out)[0], w_q1=f(swa_w_q)[0],
                  w_o1=f(swa_w_out)[0], w_kv=f(swa_w_kv), w_g=f(ffn_w_gate), w_u=f(ffn_w_up), w_d=f(ffn_w_down),
                  w_pg=f(ple_w_gate), w_pp=f(ple_w_proj))
    in_maps = []
    for c in range(8):
        b, hf = c // 2, c % 2
        if hf == 1:
            xk = x_prompt[b]
            valid = np.ones(NBT, np.float32)
        else:
            xk = np.concatenate([np.zeros((32 * 128, D), np.float32), x_prompt[b, :4096]], axis=0)
            valid = np.ones(NBT, np.float32); valid[:32] = 0.0
        pown = np.zeros((2, NQS * 128, 256), np.float32)
        if hf == 1:
            pown[:, :33 * 128] = p_prompt[:, b, 31 * 128:]
        else:
            pown[:, 128:33 * 128] = p_prompt[:, b, :4096]
        pown[:, 33 * 128:33 * 128 + 16] = p_sample[:, c]
        xs = np.zeros((128, D), np.float32); xs[:16] = x_sample[c]
        swk = np.zeros(NQS, np.float32)
        if hf == 0:
            swk[1] = NEG
        m = dict(common)
        m.update(xk=np.ascontiguousarray(xk), xs=xs, pown=pown,
                 ck=f(cache_fox_k)[0, c].reshape(2048, D), cv=f(cache_fox_v)[0, c].reshape(2048, D),
                 clf=f(cache_fox_logf)[0, c], cswk=f(cache_swa_k)[c].reshape(128, 256), cswv=f(cache_swa_v)[c].reshape(128, 256),
                 negvalid=-valid, kill=(1.0 - valid) * 30000.0, swk=swk)
        in_maps.append(m)
    res = run_bass_kernel_spmd(nc, in_maps, core_ids=list(range(8)))
    R = res.results
    y_prompt = np.zeros((4, 8192, D), np.float32); y_sample = np.zeros((8, 16, D), np.float32)
    fkp = np.zeros((1, 4, 8192, NH, HD), np.float32); fvp = np.zeros_like(fkp); flp = np.zeros((1, 4, 8192, NH), np.float32)
    fks = np.zeros((1, 8, 16, NH, HD), np.float32); fvs = np.zeros_like(fks); fls = np.zeros((1, 8, 16, NH), np.float32)
    skp = np.zeros((4, 128, 4, HD), np.float32); svp = np.zeros_like(skp)
    sks = np.zeros((8, 128, 4, HD), np.float32); svs = np.zeros_like(sks)
    for c in range(8):
        b, hf = c // 2, c % 2
        r = R[c]
        sl = slice(hf * 4096, (hf + 1) * 4096)
        y_prompt[b, sl] = r['y'][128:33 * 128]
        y_sample[c] = r['y'][33 * 128:33 * 128 + 16]
        fkp[0, b, sl] = r['fk'][128:33 * 128].reshape(4096, NH, HD)
        fvp[0, b, sl] = r['fv'][128:33 * 128].reshape(4096, NH, HD)
        flp[0, b, sl] = r['flf'][128:33 * 128]
        fks[0, c] = r['fk'][33 * 128:33 * 128 + 16].reshape(16, NH, HD)
        fvs[0, c] = r['fv'][33 * 128:33 * 128 + 16].reshape(16, NH, HD)
        fls[0, c] = r['flf'][33 * 128:33 * 128 + 16]
        if hf == 1:
            skp[b] = r['kvo'][32 * 128:33 * 128, 0:256].reshape(128, 4, HD)
            svp[b] = r['kvo'][32 * 128:33 * 128, 256:512].reshape(128, 4, HD)
        sks[c] = r['swks'].reshape(128, 4, HD)
        svs[c] = r['swvs'].reshape(128, 4, HD)
    return (y_prompt, y_sample, fkp, fvp, flp, fks, fvs, fls, skp, svp, sks, svs)
```
